# Optimizing a Trainium2 kernel written in Bass

```python
import jax, jax.numpy as jnp
from jax import lax
import numpy as np

D_MODEL = 1024
BATCH = 8
SEQ = 4096
DEPTH = 2

HEAD_DIM = 64
A_HEADS = 6
DILATED_PATTERNS = ((128, 1), (512, 4), (2048, 16))
CONV_CH = 256
CONV_K = 3
C_Q_HEADS = 6
C_KV_HEADS = 2
C_GROUP = C_Q_HEADS // C_KV_HEADS
C_WINDOW = 128
BLOCK = 128
D_FF = 4 * D_MODEL
EPS = 1e-6

A_WIDTH = A_HEADS * HEAD_DIM
C_WIDTH = C_Q_HEADS * HEAD_DIM
KV_WIDTH = C_KV_HEADS * HEAD_DIM
MIX_WIDTH = A_WIDTH + CONV_CH + C_WIDTH
IN_SPLITS = (A_WIDTH, A_WIDTH, A_WIDTH, CONV_CH, CONV_CH, CONV_CH, C_WIDTH, KV_WIDTH, KV_WIDTH)
IN_WIDTH = sum(IN_SPLITS)
SPLIT_POINTS = tuple(int(p) for p in np.cumsum(IN_SPLITS)[:-1])

kernel_name = "hymba_dilated_conv_swa_sink_trunk"


def rms_normalize(t):
    t32 = t.astype(jnp.float32)
    return (t32 * lax.rsqrt(jnp.mean(t32 * t32, axis=-1, keepdims=True) + EPS)).astype(t.dtype)


def rmsnorm(t, g):
    return rms_normalize(t) * g


def banded_attention(q, k, v, max_dist, sink_logits=None):
    n, L, hkv, g, dh = q.shape
    nb = -(-L // BLOCK)
    lp = nb * BLOCK
    pad = lp - L
    q = jnp.pad(q, ((0, 0), (0, pad), (0, 0), (0, 0), (0, 0)))
    kv_pad = ((0, 0), (BLOCK, pad), (0, 0), (0, 0))
    k = jnp.pad(k, kv_pad).reshape(n, nb + 1, BLOCK, hkv, dh)
    v = jnp.pad(v, kv_pad).reshape(n, nb + 1, BLOCK, hkv, dh)
    k2 = jnp.concatenate([k[:, :-1], k[:, 1:]], axis=2)
    v2 = jnp.concatenate([v[:, :-1], v[:, 1:]], axis=2)
    qb = q.reshape(n, nb, BLOCK, hkv, g, dh)
    s = jnp.einsum('nbqhgd,nbkhd->nbhgqk', qb, k2).astype(jnp.float32) * (dh ** -0.5)
    qi = jnp.arange(BLOCK)[:, None]
    kj = jnp.arange(2 * BLOCK)[None, :]
    dist = BLOCK + qi - kj
    band = (dist >= 0) & (dist <= max_dist)
    first = jnp.arange(nb)[:, None, None] == 0
    mask = band[None] & ~(first & (kj < BLOCK)[None])
    s = jnp.where(mask[None, :, None, None], s, -jnp.inf)
    if sink_logits is not None:
        sink = jnp.broadcast_to(sink_logits.astype(jnp.float32)[None, None, :, :, None, None], s.shape[:-1] + (1,))
        lse = jax.nn.logsumexp(jnp.concatenate([s, sink], axis=-1), axis=-1)
    else:
        lse = jax.nn.logsumexp(s, axis=-1)
    p = jnp.exp(s - lse[..., None]).astype(v2.dtype)
    o = jnp.einsum('nbhgqk,nbkhd->nbqhgd', p, v2).reshape(n, lp, hkv, g, dh)[:, :L]
    lse = lse.transpose(0, 1, 4, 2, 3).reshape(n, lp, hkv, g)[:, :L]
    return o, lse


def to_residues(t, dil):
    b, s, h, dh = t.shape
    return t.reshape(b, s // dil, dil, h, dh).transpose(0, 2, 1, 3, 4).reshape(b * dil, s // dil, h, dh)


def from_residues(t, dil, b):
    sub = t.shape[1]
    rest = t.shape[2:]
    t = t.reshape((b, dil, sub) + rest)
    t = jnp.moveaxis(t, 1, 2)
    return t.reshape((b, sub * dil) + rest)


def dilated_attention(q, k, v):
    b = q.shape[0]
    outs, lses = [], []
    for window, dil in DILATED_PATTERNS:
        o, lse = banded_attention(to_residues(q, dil)[:, :, :, None], to_residues(k, dil),
                                  to_residues(v, dil), window // dil)
        outs.append(from_residues(o[:, :, :, 0], dil, b))
        lses.append(from_residues(lse[..., 0], dil, b))
    wts = jax.nn.softmax(jnp.stack(lses), axis=0)
    return jnp.einsum('pbsh,pbshd->bshd', wts.astype(q.dtype), jnp.stack(outs))


def short_gated_conv(gate_b, gate_c, xb, w):
    s = xb.shape[1]
    u = gate_c * xb
    up = jnp.pad(u, ((0, 0), (CONV_K - 1, 0), (0, 0)))
    y = sum(w[i] * up[:, i:i + s] for i in range(CONV_K))
    return gate_b * y


def setup_inputs(seed: int = 0) -> dict:
    key = jax.random.key(seed)
    ks = jax.random.split(key, 12)
    nrm = jax.random.normal
    x = nrm(ks[0], (BATCH, SEQ, D_MODEL), jnp.float32)
    w_in = nrm(ks[1], (DEPTH, D_MODEL, IN_WIDTH), jnp.float32) * D_MODEL ** -0.5
    conv_w = nrm(ks[2], (DEPTH, CONV_K, CONV_CH), jnp.float32) * CONV_K ** -0.5
    sinks = nrm(ks[3], (DEPTH, C_KV_HEADS, C_GROUP), jnp.float32) * 0.5
    g_mix = 1.0 + 0.02 * nrm(ks[4], (DEPTH, D_MODEL), jnp.float32)
    g_group = 1.0 + 0.02 * nrm(ks[5], (DEPTH, MIX_WIDTH), jnp.float32)
    w_o = nrm(ks[6], (DEPTH, MIX_WIDTH, D_MODEL), jnp.float32) * MIX_WIDTH ** -0.5
    g_mlp = 1.0 + 0.02 * nrm(ks[7], (DEPTH, D_MODEL), jnp.float32)
    w_ff_in = nrm(ks[8], (DEPTH, D_MODEL, D_FF), jnp.float32) * D_MODEL ** -0.5
    w_ff_out = nrm(ks[9], (DEPTH, D_FF, D_MODEL), jnp.float32) * D_FF ** -0.5
    g_final = 1.0 + 0.02 * nrm(ks[10], (D_MODEL,), jnp.float32)
    return {"x": x, "w_in": w_in, "conv_w": conv_w, "sinks": sinks, "g_mix": g_mix,
            "g_group": g_group, "w_o": w_o, "g_mlp": g_mlp, "w_ff_in": w_ff_in,
            "w_ff_out": w_ff_out, "g_final": g_final}


def reference(x, w_in, conv_w, sinks, g_mix, g_group, w_o, g_mlp, w_ff_in, w_ff_out, g_final):
    b, s, _ = x.shape
    for l in range(DEPTH):
        h = rmsnorm(x, g_mix[l])
        z = jnp.einsum('bsd,de->bse', h, w_in[l])
        qa, ka, va, gb, gc, xb, qc, kc, vc = jnp.split(z, SPLIT_POINTS, axis=-1)
        ya = dilated_attention(qa.reshape(b, s, A_HEADS, HEAD_DIM),
                               ka.reshape(b, s, A_HEADS, HEAD_DIM),
                               va.reshape(b, s, A_HEADS, HEAD_DIM)).reshape(b, s, A_WIDTH)
        yb = short_gated_conv(gb, gc, xb, conv_w[l])
        oc, _ = banded_attention(qc.reshape(b, s, C_KV_HEADS, C_GROUP, HEAD_DIM),
                                 kc.reshape(b, s, C_KV_HEADS, HEAD_DIM),
                                 vc.reshape(b, s, C_KV_HEADS, HEAD_DIM),
                                 C_WINDOW - 1, sinks[l])
        yc = oc.reshape(b, s, C_WIDTH)
        y = jnp.concatenate([rms_normalize(ya), rms_normalize(yb), rms_normalize(yc)], axis=-1) * g_group[l]
        x = x + jnp.einsum('bse,ed->bsd', y, w_o[l])
        h2 = rmsnorm(x, g_mlp[l])
        a = jnp.square(jax.nn.relu(jnp.einsum('bsd,df->bsf', h2, w_ff_in[l])))
        x = x + jnp.einsum('bsf,fd->bsd', a, w_ff_out[l])
    return rmsnorm(x, g_final)
```

```python
from contextlib import ExitStack
import os
import numpy as np
import concourse.bass as bass
import concourse.mybir as mybir
from concourse.bass_utils import run_bass_kernel_spmd

F32 = mybir.dt.float32
BF16 = mybir.dt.bfloat16
AF = mybir.ActivationFunctionType
ALU = mybir.AluOpType

S = 4096
D = 1024
DEPTH = 2
NCORES = 8
A1CAP = 28672
EPS = 1e-6
ENGS = ("pe", "act", "dve", "pool", "sp")


class Sem:
    def __init__(self, h, name):
        self.h = h
        self.name = name
        self.count = 0


class Plan:
    def __init__(self):
        self.nc = bass.Bass("TRN2", target_bir_lowering=False)
        self.stack = ExitStack()
        self.ops = {e: [] for e in ENGS}
        self.waited = {e: {} for e in ENGS}
        self.prog = {e: self.sem("prog_" + e) for e in ENGS[:4]}
        self.dsem = {}
        self.w = {}
        self.r = {}
        self.pending_dma = {}
        self.last_ev = {}
        self.pe_rg = None
        self.sep_fn = None

    def sem(self, name):
        return Sem(self.stack.enter_context(self.nc.semaphore(name)), name)

    def sbuf(self, name, shape, dt):
        return self.stack.enter_context(self.nc.sbuf_tensor(name, list(shape), dt))

    def psum(self, name, shape, dt=F32):
        return self.stack.enter_context(self.nc.psum_tensor(name, list(shape), dt))

    def dram(self, name, shape, dt, kind="Internal"):
        return self.nc.dram_tensor(name, list(shape), dt, kind=kind).ap()

    def op(self, eng, fn, waits=(), inc=None, n=1, rg=None):
        if eng == "pe" and fn is not None:
            if rg is not None and self.pe_rg is not None and rg != self.pe_rg:
                self.ops["pe"].append(([], self.sep_fn, None))
            self.pe_rg = rg
        wl = []
        wd = self.waited[eng]
        for w in waits:
            if w is None:
                continue
            s, v = w
            if wd.get(s.name, 0) >= v:
                continue
            wd[s.name] = v
            wl.append((s.h, v))
        ev = None
        incs = None
        if inc is not None and inc is not False:
            s = self.prog[eng] if inc is True else inc
            s.count += n
            incs = (s.h, n)
            ev = (s, s.count)
            self.last_ev[eng] = ev
        self.ops[eng].append((wl, fn, incs))
        return ev

    def _waits(self, reads, writes):
        out = []
        for b in reads:
            if b in self.w:
                out.append(self.w[b])
        for b in writes:
            if b in self.w:
                out.append(self.w[b])
            out.extend(self.r.get(b, {}).values())
        return out

    def _commit(self, ev, reads, writes):
        for b in reads:
            d = self.r.setdefault(b, {})
            cur = d.get(ev[0].name)
            if cur is None or cur[1] < ev[1]:
                d[ev[0].name] = ev
        for b in writes:
            self.w[b] = ev
            self.r[b] = {}

    def do(self, eng, fn, reads=(), writes=(), extra=()):
        ev = self.op(eng, fn, waits=self._waits(reads, writes) + list(extra), inc=True)
        self._commit(ev, reads, writes)
        return ev

    def dma(self, eng, out, in_, semkey, reads=(), writes=()):
        if semkey not in self.dsem:
            self.dsem[semkey] = self.sem("d_" + semkey)
        s = self.dsem[semkey]
        ev = self.op(eng, lambda e, o=out, i=in_: e.dma_start(out=o, in_=i),
                     waits=self._waits(reads, writes), inc=s, n=16)
        self._commit(ev, reads, writes)
        self.pending_dma[s.name] = ev
        return ev

    def mm_group(self, mms, reads, writes, start_first=True, stop_last=True, inc=True):
        waits = self._waits(reads, writes)
        n = len(mms)
        ev = None
        for i, (o, l, r) in enumerate(mms):
            st = start_first and i == 0
            sp = stop_last and i == n - 1
            fn = (lambda e, o=o, l=l, r=r, st=st, sp=sp: e.matmul(o, lhsT=l, rhs=r, start=st, stop=sp))
            last = (i == n - 1) and inc
            ev = self.op("pe", fn, waits=waits if i == 0 else (), inc=True if last else None)
        if inc:
            self._commit(ev, reads, writes)
        return ev

    def pe_raw(self, fn, waits=(), inc=None):
        return self.op("pe", fn, waits=waits, inc=inc)

    def act(self, out, in_, func, reads, writes, scale=None, bias=None, accum=None):
        kw = {}
        if scale is not None:
            kw["scale"] = scale
        if bias is not None:
            kw["bias"] = bias
        if accum is not None:
            kw["accum_out"] = accum
        return self.do("act", lambda e: e.activation(out=out, in_=in_, func=func, **kw), reads, writes)

    def tcopy(self, eng, out, in_, reads, writes):
        return self.do(eng, lambda e: e.tensor_copy(out=out, in_=in_), reads, writes)

    def tt(self, out, in0, in1, op, reads, writes, eng="dve"):
        return self.do(eng, lambda e: e.tensor_tensor(out=out, in0=in0, in1=in1, op=op), reads, writes)

    def tsmul(self, out, in0, sc, reads, writes, eng="dve"):
        return self.do(eng, lambda e: e.tensor_scalar_mul(out=out, in0=in0, scalar1=sc), reads, writes)

    def stt(self, out, in0, sc, in1, op0, op1, reads, writes, eng="dve"):
        return self.do(eng, lambda e: e.scalar_tensor_tensor(out=out, in0=in0, scalar=sc, in1=in1, op0=op0, op1=op1),
                       reads, writes)

    def memset(self, eng, ap, val, writes):
        return self.do(eng, lambda e: e.memset(ap, val), (), writes)

    def barrier(self, dummy):
        evs = []
        evs.append(self.do("act", lambda e: e.activation(out=dummy["act"][:], in_=dummy["src"][:], func=AF.Copy),
                           ("epst",), ("dummy_act",)))
        evs.append(self.do("dve", lambda e: e.memset(dummy["dve"][:], 0.0), (), ("dummy_dve",)))
        if "pe" in self.last_ev:
            evs.append(self.last_ev["pe"])
        evs += list(self.pending_dma.values())
        self.pending_dma = {}
        for eng in ENGS:
            self.op(eng, None, waits=evs)

    def build(self):
        nc = self.nc
        with nc.Block() as block:
            def mk(engname):
                def body(e):
                    for wl, fn, incs in self.ops[engname]:
                        for h, v in wl:
                            e.wait_ge(h, v)
                        if fn is None:
                            continue
                        ins = fn(e)
                        if incs is not None:
                            ins.then_inc(incs[0], incs[1])
                return body
            block.tensor(mk("pe"))
            block.scalar(mk("act"))
            block.vector(mk("dve"))
            block.gpsimd(mk("pool"))
            block.sync(mk("sp"))
        self.stack.close()
        return nc


class Arena:
    def __init__(self, t, cap, name):
        self.t = t
        self.cap = cap
        self.top = 0
        self.name = name

    def alloc(self, nelem_bf16, dt=BF16):
        n = (nelem_bf16 + 15) // 16 * 16
        assert self.top + n <= self.cap, (self.name, self.top, n, self.cap)
        v = self.t[:, self.top:self.top + n]
        self.top += n
        if dt == F32:
            v = v.bitcast(F32)
        return v

    def f32(self, n):
        return self.alloc(2 * n, F32)[:, 0:n]

    def b16(self, n):
        return self.alloc(n)[:, 0:n]


def build_program(nlayers=DEPTH, debug=False, stop_after=None):
    P = Plan()
    nc = P.nc
    ext_in = "ExternalInput"
    dbg_kind = "ExternalOutput" if debug else "Internal"

    x_in = P.dram("x", [S, D], F32, ext_in)
    w_in_d = P.dram("w_in", [DEPTH, D, 2560], F32, ext_in)
    convw_d = P.dram("convw", [DEPTH, 128, 2, 3], F32, ext_in)
    sinks_d = P.dram("sinks", [DEPTH, 6], F32, ext_in)
    g_mix_d = P.dram("g_mix", [DEPTH, 128, 8], F32, ext_in)
    g_grp_d = P.dram("g_group", [DEPTH, 128, 8], F32, ext_in)
    g_mlp_d = P.dram("g_mlp", [DEPTH, 128, 8], F32, ext_in)
    w_o_d = P.dram("w_o", [DEPTH, D, D], F32, ext_in)
    w1_d = P.dram("w_ff_in", [DEPTH, D, 4096], F32, ext_in)
    w2_d = P.dram("w_ff_out", [DEPTH, 4096, D], F32, ext_in)
    g_fin_d = P.dram("g_final", [D], F32, ext_in)
    cst_d = P.dram("consts", [128, 768], F32, ext_in)
    out_d = P.dram("out", [S, D], F32, "ExternalOutput")

    xs = P.dram("xs", [S, D], F32, dbg_kind)
    Va = P.dram("Va", [S, 390], BF16, dbg_kind)
    Vc = P.dram("Vc", [S, 130], BF16, dbg_kind)
    Oa = P.dram("Oa", [3, S, 390], F32, dbg_kind)
    w1b = P.dram("w1b", [DEPTH, D, 4096], BF16)
    w2b = P.dram("w2b", [DEPTH, 4096, D], BF16)

    A0t = P.sbuf("A0", [128, 65536], BF16)
    A1t = P.sbuf("A1", [128, A1CAP], BF16)
    stage = [P.sbuf("stage%d" % i, [128, 1024], F32) for i in range(2)]
    sb16 = [P.sbuf("sb16_%d" % i, [128, 1024], BF16) for i in range(2)]
    cst_f = P.sbuf("cst_f", [128, 768], F32)
    cst_b = P.sbuf("cst_b", [128, 1664], BF16)
    onesb = P.sbuf("onesb", [128, 128], BF16)
    epst = P.sbuf("epst", [128, 1], F32)
    gfin = P.sbuf("gfin", [128, D], F32)
    gt = P.sbuf("gt", [128, DEPTH * 3 * 8], F32)
    cw = P.sbuf("cw", [128, DEPTH * 6], F32)
    sk = P.sbuf("sk", [128, DEPTH * 6], F32)
    esk = P.sbuf("esk", [128, DEPTH * 6], F32)
    dummy = {"act": P.sbuf("dmy_a", [128, 1], F32), "dve": P.sbuf("dmy_v", [128, 1], F32), "src": epst}
    st = {}
    for nm, w_ in (("ssq", 2), ("lnv", 2), ("rstd", 2), ("rd", 12), ("dc", 12), ("rdc", 12), ("ss2", 4), ("ln2", 4), ("rs2", 4),
                   ("fss", 2), ("fln", 2), ("frs", 2), ("ssa", 1), ("lna", 1), ("rsa", 1), ("ssc", 1), ("lnc", 1), ("rsc", 1)):
        st[nm] = P.sbuf("st_" + nm, [128, w_], F32)

    pT = P.psum("pT", [128, 1024], BF16)
    pball = P.psum("pball", [128, 7 * 512], F32)
    pb = [pball[:, i * 512:(i + 1) * 512] for i in range(7)]

    identb = cst_b[:, 0:128]
    maskA = cst_b[:, 640:1152]
    maskC = cst_b[:, 1152:1664]

    P.dma("sp", cst_f[:], cst_d[:, :], "cst", writes=("cst_f",))
    P.dma("sp", gfin[:], g_fin_d.partition_broadcast(128), "gfin", writes=("gfin",))
    for l in range(DEPTH):
        for k, gd in enumerate((g_mix_d, g_grp_d, g_mlp_d)):
            o = (l * 3 + k) * 8
            P.dma("sp", gt[:, o:o + 8], gd[l], "gt%d" % (l * 3 + k), writes=("gt%d_%d" % (l, k),))
        P.dma("sp", cw[:, l * 6:(l + 1) * 6], convw_d[l].rearrange("p c k -> p (c k)"), "cw%d" % l, writes=("cw%d" % l,))
        P.dma("sp", sk[:, l * 6:(l + 1) * 6], sinks_d[l].partition_broadcast(128), "sk%d" % l, writes=("sk%d" % l,))
    P.tcopy("dve", cst_b[:, 0:768], cst_f[:], ("cst_f",), ("cst_b0",))
    for mi in range(2):
        for hh in range(2):
            P.tcopy("dve", cst_b[:, 640 + mi * 512 + hh * 256:640 + mi * 512 + (hh + 1) * 256],
                    cst_f[:, 128 + mi * 256:384 + mi * 256], ("cst_f", "cst_b0"), ("cst_b%d" % (1 + mi * 2 + hh),))
    P.tcopy("dve", dummy["dve"][:], epst[:], ("cst_b0", "cst_b1", "cst_b2", "cst_b3", "cst_b4"), ("cst_b",))
    P.memset("dve", onesb[:], 1.0, ("onesb",))
    P.memset("dve", epst[:], EPS, ("epst",))
    P.act(esk[:], sk[:], AF.Exp, ("sk0", "sk1"), ("esk",))

    P.sep_fn = lambda e: e.matmul(pb[4][:, 448:512], lhsT=identb, rhs=cst_b[:, 0:64], start=True, stop=True)

    def gcol(l, kind, k):
        o = (l * 3 + kind) * 8 + k
        return gt[:, o:o + 1]

    def gkey(l, kind):
        return "gt%d_%d" % (l, kind)

    stage_ctr = [0]

    def load_weight(dst, src, K, N, dkey, gl=None, cols=None):
        for k in range(K):
            load_weight_k(dst, src, k, N, dkey, gl)

    def load_weight_k(dst, src, k, N, dkey, gl=None):
        if True:
            for c0 in range(0, N, 1024):
                cn = min(1024, N - c0)
                sl = stage_ctr[0] % 2
                stage_ctr[0] += 1
                skey = "stage%d" % sl
                P.dma("sp", stage[sl][:, 0:cn], src[k * 128:(k + 1) * 128, c0:c0 + cn], skey, writes=(skey,))
                if gl is not None:
                    P.tsmul(dst[:, k, c0:c0 + cn], stage[sl][:, 0:cn], gcol(gl[0], gl[1], k),
                            (skey, gkey(*gl)), (dkey,))
                else:
                    P.tcopy("dve", dst[:, k, c0:c0 + cn], stage[sl][:, 0:cn], (skey,), (dkey,))

    def precast_src(l, idx):
        if idx < 32:
            k, cb = idx // 4, idx % 4
            return (w1_d[l][k * 128:(k + 1) * 128, cb * 1024:(cb + 1) * 1024],
                    w1b[l][k * 128:(k + 1) * 128, cb * 1024:(cb + 1) * 1024], k)
        k = idx - 32
        return w2_d[l][k * 128:(k + 1) * 128, :], w2b[l][k * 128:(k + 1) * 128, :], None

    def precast_load(l, idx):
        src, dst, k = precast_src(l, idx)
        sl = stage_ctr[0] % 2
        stage_ctr[0] += 1
        P.dma("sp", stage[sl][:, :], src, "stage%d" % sl, writes=("stage%d" % sl,))
        return sl

    def precast_cast(l, idx, sl):
        src, dst, k = precast_src(l, idx)
        bs = idx % 2
        bk = "sb16_%d" % bs
        if k is not None:
            P.tsmul(sb16[bs][:, :], stage[sl][:, :], gcol(l, 2, k), ("stage%d" % sl, gkey(l, 2)), (bk,))
        else:
            P.tcopy("dve", sb16[bs][:, :], stage[sl][:, :], ("stage%d" % sl,), (bk,))
        P.dma("pool", dst, sb16[bs][:, :], bk, reads=(bk,))

    def norm_tile(xt, xkey, hb, hbkey, j):
        ssq = st["ssq"][:, j:j + 1]
        lnv = st["lnv"][:, j:j + 1]
        rstd = st["rstd"][:, j:j + 1]
        P.act(hb, xt, AF.Square, (xkey,), (hbkey, "ssq%d" % j), scale=1.0 / 32.0, accum=ssq)
        P.act(lnv, ssq, AF.Ln, ("ssq%d" % j, "epst"), ("lnv%d" % j,), bias=epst[:])
        P.act(rstd, lnv, AF.Exp, ("lnv%d" % j,), ("rstd%d" % j,), scale=-0.5)
        P.tsmul(hb, xt, rstd, (xkey, "rstd%d" % j), (hbkey,))

    def transpose8(hb, hbkey, hT, hTkey, c0):
        waits = P._waits((hbkey, "cst_b"), ("pT",))
        ev = None
        for k in range(8):
            fn = (lambda e, k=k: e.transpose(out=pT[:, k * 128:(k + 1) * 128], in_=hb[:, k * 128:(k + 1) * 128],
                                             identity=identb))
            ev = P.op("pe", fn, waits=waits if k == 0 else (), inc=True if k == 7 else None)
        P._commit(ev, (hbkey, "cst_b"), ("pT",))
        P.tcopy("dve", hT[:, :, c0:c0 + 128], pT[:, :].rearrange("p (k n) -> p k n", k=8), ("pT",), (hTkey,))

    for l in range(nlayers):
        xsrc = x_in if l == 0 else xs
        A0 = Arena(A0t, 65536, "A0")
        A1 = Arena(A1t, A1CAP, "A1")
        QaT = A0.alloc(3 * S).rearrange("p (c n) -> p c n", c=3)
        KaT = A0.alloc(3 * S).rearrange("p (c n) -> p c n", c=3)
        QcT = A0.alloc(3 * S).rearrange("p (c n) -> p c n", c=3)
        KcT = A0.alloc(S)
        ybT = A0.alloc(2 * S).rearrange("p (c n) -> p c n", c=2)
        a0_mark = A0.top

        Win = A1.alloc(8 * 2560).rearrange("p (k n) -> p k n", k=8)
        hTs = [A1.alloc(8 * 512).rearrange("p (k n) -> p k n", k=8) for _ in range(2)]
        xin = [A0.f32(1024) for _ in range(2)]
        hbs = [A0.b16(1024), A0.b16(1024)]
        gcs = A0.f32(512)
        u = [A0.f32(516) for _ in range(2)]
        acc = A0.f32(512)
        yb = [A0.f32(512) for _ in range(2)]
        sq = [A0.b16(512) for _ in range(2)]
        rsb = A0.f32(512)
        Vst = [A0.b16(520).rearrange("p (h d) -> p h d", h=8) for _ in range(2)]

        load_weight(Win, w_in_d[l], 8, 2560, "Win", gl=(l, 0))
        for j in range(2):
            P.memset("dve", u[j][:, 0:2], 0.0, ("u%d" % j,))
            P.memset("dve", Vst[j][:, :, 64:65], 1.0, ("Vst%d" % j,))

        def p1_dma(T, s_):
            g = T * 4 + s_
            sl = g % 2
            xk = "xin%d" % sl
            P.dma("sp", xin[sl][:], xsrc[g * 128:(g + 1) * 128, :], xk, writes=(xk,))

        def p1_norm(T, s_):
            g = T * 4 + s_
            sl = g % 2
            norm_tile(xin[sl][:], "xin%d" % sl, hbs[sl], "hb%d" % sl, sl)

        def p1_tr(T, s_):
            g = T * 4 + s_
            sl = g % 2
            transpose8(hbs[sl], "hb%d" % sl, hTs[T % 2], "hT%d" % (T % 2), s_ * 128)

        for s_ in range(4):
            p1_dma(0, s_)
            p1_norm(0, s_)
            p1_tr(0, s_)
        pc_next = [0]
        pc_pending = [None]
        for T in range(8):
            t0 = T * 512
            hT = hTs[T % 2]
            hTk = "hT%d" % (T % 2)
            gctr = [0]

            def tick():
                gi = gctr[0]
                gctr[0] += 1
                if gi in (1, 3, 6, 8, 11, 13, 16, 18):
                    if pc_pending[0] is not None:
                        precast_cast(l, pc_pending[0][0], pc_pending[0][1])
                        pc_pending[0] = None
                    if pc_next[0] < 64:
                        pc_pending[0] = (pc_next[0], precast_load(l, pc_next[0]))
                        pc_next[0] += 1
                if T + 1 >= 8:
                    return
                if gi in (0, 2, 5, 8):
                    p1_dma(T + 1, (0, 2, 5, 8).index(gi))
                if gi in (1, 4, 7, 10):
                    p1_norm(T + 1, (gi - 1) // 3)
                if gi in (4, 7, 10, 13):
                    p1_tr(T + 1, (gi - 4) // 3)
            def fm_chunk(ci, bank):
                bk = "pb%d" % bank
                mms = [(pb[bank][:, :], Win[:, k, ci * 128:(ci + 1) * 128], hT[:, k, :]) for k in range(8)]
                P.mm_group(mms, ("Win", hTk), (bk,))
                tick()
                return bk
            zi = 0
            for ci, dst, dkey in ([(c, QaT[:, c, t0:t0 + 512], "QaT") for c in range(3)] +
                                  [(3 + c, KaT[:, c, t0:t0 + 512], "KaT") for c in range(3)] +
                                  [(12 + c, QcT[:, c, t0:t0 + 512], "QcT") for c in range(3)] +
                                  [(15, KcT[:, t0:t0 + 512], "KcT")]):
                bank = zi % 3
                zi += 1
                bk = fm_chunk(ci, bank)
                P.act(dst, pb[bank][:, :], AF.Copy, (bk,), (dkey,))
            for j in range(2):
                bank = zi % 3; zi += 1
                bk = fm_chunk(8 + j, bank)
                P.act(gcs, pb[bank][:, :], AF.Copy, (bk,), ("gcs",))
                bank = zi % 3; zi += 1
                bk = fm_chunk(10 + j, bank)
                uk = "u%d" % j
                P.tt(u[j][:, 2:514], pb[bank][:, :], gcs, ALU.mult, (bk, "gcs"), (uk,))
                wofs = l * 6 + j * 3
                P.tsmul(acc, u[j][:, 0:512], cw[:, wofs:wofs + 1], (uk, "cw%d" % l), ("acc",))
                P.stt(acc, u[j][:, 1:513], cw[:, wofs + 1:wofs + 2], acc, ALU.mult, ALU.add, (uk, "cw%d" % l, "acc"), ("acc",))
                P.stt(acc, u[j][:, 2:514], cw[:, wofs + 2:wofs + 3], acc, ALU.mult, ALU.add, (uk, "cw%d" % l, "acc"), ("acc",))
                bank = zi % 3; zi += 1
                bk = fm_chunk(6 + j, bank)
                P.tt(yb[j], pb[bank][:, :], acc, ALU.mult, (bk, "acc"), ("yb%d" % j,))
                P.act(sq[j], yb[j], AF.Square, ("yb%d" % j,), ("sq%d" % j,))
                P.tcopy("dve", u[j][:, 0:2], u[j][:, 512:514], (uk,), (uk,))
            P.mm_group([(pb[5][:, :], onesb[:, :], sq[0]), (pb[5][:, :], onesb[:, :], sq[1])],
                       ("onesb", "sq0", "sq1"), ("pb5",))
            tick()
            P.act(acc, pb[5][:, :], AF.Ln, ("pb5", "epst"), ("acc",), scale=1.0 / 256.0, bias=epst[:])
            P.act(rsb, acc, AF.Exp, ("acc",), ("rsb",), scale=-0.5)
            for j in range(2):
                P.tt(ybT[:, j, t0:t0 + 512], yb[j], rsb, ALU.mult, ("yb%d" % j, "rsb"), ("ybT",))
            for s in range(4):
                g = T * 4 + s
                bank = 3 + (g % 2)
                bk = "pb%d" % bank
                mms = [(pb[bank][:, :], hT[:, k, s * 128:(s + 1) * 128], Win[:, k, 2048:2560]) for k in range(8)]
                P.mm_group(mms, ("Win", hTk), (bk,))
                tick()
                vs = g % 2
                vk = "Vst%d" % vs
                P.tcopy("dve", Vst[vs][:, :, 0:64], pb[bank][:, :].rearrange("p (h d) -> p h d", h=8), (bk,), (vk,))
                P.dma("pool", Va[g * 128:(g + 1) * 128, :], Vst[vs][:, 0:6, :].rearrange("p h d -> p (h d)"), vk, reads=(vk,))
                P.dma("pool", Vc[g * 128:(g + 1) * 128, :], Vst[vs][:, 6:8, :].rearrange("p h d -> p (h d)"), vk, reads=(vk,))
        if pc_pending[0] is not None:
            precast_cast(l, pc_pending[0][0], pc_pending[0][1])
        assert pc_next[0] == 64
        P.barrier(dummy)
        if stop_after == (l, 1):
            break

        A0.top = a0_mark
        A1.top = 0
        Wo = A1.alloc(8 * 1024).rearrange("p (k n) -> p k n", k=8)
        Vcall = A1.alloc(32 * 130).rearrange("p (j h d) -> p j h d", j=32, h=2)
        PT = [A1.alloc(3 * 512).rearrange("p (c n) -> p c n", c=3) for _ in range(3)]
        Vr = [A1.alloc(392)[:, 0:390].rearrange("p (h d) -> p h d", h=6) for _ in range(4)]
        Ost = [A1.f32(390) for _ in range(2)]
        xo = [A1.f32(1024) for _ in range(2)]
        OaL = [A0.f32(3 * 390).rearrange("p (n c) -> p n c", n=3) for _ in range(2)]
        N1 = A0.f32(390)
        Ns = A0.f32(390)
        ya = [A0.f32(384) for _ in range(2)]
        yc = [A0.f32(384) for _ in range(2)]
        y16 = [A0.b16(768) for _ in range(2)]
        jk = [A0.b16(384) for _ in range(2)]
        yT = [A0.b16(768).rearrange("p (c n) -> p c n", c=6) for _ in range(2)]
        xres = [A1.f32(1024) for _ in range(2)]
        PT.append(A0.alloc(3 * 512).rearrange("p (c n) -> p c n", c=3))

        P.dma("sp", Vcall.rearrange("p j h d -> p j (h d)"), Vc.rearrange("(j p) c -> p j c", p=128), "Vcall",
              writes=("Vcall",))

        sctr = [0]
        octr = [0]
        vctr = [0]
        pctr = [0]

        def score_mm(bank, hh, kt, qt, nq):
            bk = "pb%d" % bank
            o = pb[bank][:, hh * 256:hh * 256 + nq]
            ev = P.op("pe", (lambda e, o=o, l_=kt, r_=qt: e.matmul(o, lhsT=l_, rhs=r_, start=True, stop=True)),
                      waits=P._waits(("QK",), (bk,)), inc=True, rg=hh)
            P._commit(ev, ("QK",), (bk,))

        def score_exp(bank, mask, nq, pslot, c, meng="dve"):
            bk = "pb%d" % bank
            pk = "PT%d_%d" % (pslot, c)
            src = pb[bank][:, :].rearrange("p (h n) -> p h n", h=2)[:, :, 0:nq]
            dst = PT[pslot][:, c, :].rearrange("p (h n) -> p h n", h=2)[:, :, 0:nq]
            P.act(dst, src, AF.Exp, (bk,), (pk,), scale=0.125)
            P.tt(dst, dst, mask.rearrange("p (h n) -> p h n", h=2)[:, :, 0:nq], ALU.mult, (pk, "cst_b"), (pk,), eng=meng)

        SB = (0, 1, 2, 3)
        items = []
        for pi, dil in enumerate((1, 4, 16)):
            nb = 32 // dil
            for r in range(dil):
                for kb in range(nb):
                    items.append((pi, dil, r, kb, nb))
        slots = {}

        def a_info(i):
            pi, dil, r, kb, nb = items[i]
            nq = 256 if kb < nb - 1 else 128
            k0 = r + dil * 128 * kb
            return pi, dil, r, kb, nb, nq, k0

        def a_loadv(i):
            pi, dil, r, kb, nb, nq, k0 = a_info(i)
            Vav = Va.rearrange("(j p dl) c -> dl p j c", p=128, dl=dil)
            vk = "Vr%d" % (i % 4)
            P.dma("sp", Vr[i % 4].rearrange("p h d -> p (h d)"), Vav[r][:, kb, :], vk, writes=(vk,))

        def a_scores_half(i, hh):
            pi, dil, r, kb, nb, nq, k0 = a_info(i)
            for c in range(3):
                score_mm(SB[(3 * i + c) % 4], hh, KaT[hh * 64:(hh + 1) * 64, c, k0:k0 + dil * 127 + 1:dil],
                         QaT[hh * 64:(hh + 1) * 64, c, k0:k0 + dil * (nq - 1) + 1:dil], nq)

        def a_exp(i):
            pi, dil, r, kb, nb, nq, k0 = a_info(i)
            for c in range(3):
                score_exp(SB[(3 * i + c) % 4], maskA, nq, i % 4, c)

        def a_pv(i, part):
            pi, dil, r, kb, nb = items[i]
            Oav = Oa[pi].rearrange("(j p dl) c -> dl p j c", p=128, dl=dil)
            vs, ps = i % 4, i % 4
            pvs, pps = (i - 1) % 4, (i - 1) % 4
            ob = 4 + (i % 2)
            obk = "pb%d" % ob
            rd_ = ["Vr%d" % vs] + ["PT%d_%d" % (ps, c) for c in range(3)]
            if kb > 0:
                rd_ += ["Vr%d" % pvs] + ["PT%d_%d" % (pps, c) for c in range(3)]
            waits = P._waits(rd_, (obk,))
            ev = None
            first = True
            hs = (0, 1, 2) if part == 0 else (3, 4, 5)
            for h in hs:
                c, hh = h // 2, h % 2
                o = pb[ob][:, h * 65:(h + 1) * 65]
                if kb > 0:
                    P.op("pe", (lambda e, o=o, l_=PT[pps][:, c, hh * 256 + 128:hh * 256 + 256], r_=Vr[pvs][:, h, :]:
                                e.matmul(o, lhsT=l_, rhs=r_, start=True, stop=False)),
                         waits=waits if first else ())
                    first = False
                ev = P.op("pe", (lambda e, o=o, l_=PT[ps][:, c, hh * 256:hh * 256 + 128], r_=Vr[vs][:, h, :], st_=(kb == 0):
                                 e.matmul(o, lhsT=l_, rhs=r_, start=st_, stop=True)),
                          waits=waits if first else (), inc=True if h == hs[-1] else None)
                first = False
            P._commit(ev, rd_, (obk,))
            if part == 1:
                os_ = i % 2
                ok = "Ost%d" % os_
                P.tcopy("dve", Ost[os_], pb[ob][:, 0:390], (obk,), (ok,))
                P.dma("pool", Oav[r][:, kb, :], Ost[os_], ok, reads=(ok,))

        LA = 2
        for j in range(LA):
            a_loadv(j)
            a_scores_half(j, 0)
            a_scores_half(j, 1)
            a_exp(j)
        for i in range(len(items)):
            nxt = i + LA < len(items)
            if i % 8 == 2 and i // 8 < 8:
                load_weight_k(Wo, w_o_d[l], i // 8, 1024, "Wo", gl=(l, 1))
            if nxt:
                a_loadv(i + LA)
                a_scores_half(i + LA, 0)
            a_pv(i, 0)
            if nxt:
                a_scores_half(i + LA, 1)
                a_exp(i + LA)
            a_pv(i, 1)
        pctr[0] = len(items)
        octr[0] = len(items)
        P.barrier(dummy)
        if stop_after == (l, 2):
            break

        CB = (0, 1, 6)
        PB = 96
        W1pre = A0t[:, 0:8 * 4096].rearrange("p (k n) -> p k n", k=8)

        def w1_prefetch(kk):
            P.dma("sp", W1pre[:, kk:kk + 1, :], w1b[l][kk * 128:(kk + 1) * 128, :].rearrange("(k p) n -> p k n", p=128),
                  "W1pre", writes=("W1",))

        def f0(q):
            q0 = q * 128
            nq = 256 if q < 31 else 128
            sl = q % 2
            lk = "OaL%d" % sl
            P.dma("sp", OaL[sl], Oa[:, q0:q0 + 128, :].rearrange("n t c -> t n c"), lk, writes=(lk,))
            for hh in range(2):
                for c in range(3):
                    score_mm(CB[c], hh, KcT[hh * 64:(hh + 1) * 64, q0:q0 + 128], QcT[hh * 64:(hh + 1) * 64, c, q0:q0 + nq], nq)
            for c in range(3):
                score_exp(CB[c], maskC, nq, (PB + q) % 4, c, meng="dve")

        def f1(q):
            ps = (PB + q) % 4
            pps = (PB + q - 1) % 4
            ob = 4 + (q % 2)
            obk = "pb%d" % ob
            rd_ = ["Vcall"] + ["PT%d_%d" % (ps, c) for c in range(3)]
            if q > 0:
                rd_ += ["PT%d_%d" % (pps, c) for c in range(3)]
            waits = P._waits(rd_, (obk,))
            ev = None
            first = True
            for h in range(6):
                hh, c = h // 3, h % 3
                o = pb[ob][:, h * 65:(h + 1) * 65]
                if q > 0:
                    P.op("pe", (lambda e, o=o, l_=PT[pps][:, c, hh * 256 + 128:hh * 256 + 256], r_=Vcall[:, q - 1, hh, :]:
                                e.matmul(o, lhsT=l_, rhs=r_, start=True, stop=False)),
                         waits=waits if first else ())
                    first = False
                ev = P.op("pe", (lambda e, o=o, l_=PT[ps][:, c, hh * 256:hh * 256 + 128], r_=Vcall[:, q, hh, :], st_=(q == 0):
                                 e.matmul(o, lhsT=l_, rhs=r_, start=st_, stop=True)),
                          waits=waits if first else (), inc=True if h == 5 else None)
                first = False
            P._commit(ev, rd_, (obk,))
            sl = q % 2
            lk = "OaL%d" % sl
            P.tt(N1, OaL[sl][:, 0, :], OaL[sl][:, 1, :], ALU.add, (lk,), ("N1",), eng="dve")
            P.tt(Ns, N1, OaL[sl][:, 2, :], ALU.add, (lk, "N1"), ("Ns",), eng="dve")
            Nv = Ns.rearrange("p (h d) -> p h d", h=6)
            rd = st["rd"][:, sl * 6:(sl + 1) * 6]
            P.do("dve", lambda e: e.reciprocal(out=rd, in_=Nv[:, :, 64]), ("Ns",), ("rd%d" % sl,))
            P.tt(ya[sl].rearrange("p (h d) -> p h d", h=6), Nv[:, :, 0:64],
                 rd.unsqueeze(2).to_broadcast([128, 6, 64]), ALU.mult, ("Ns", "rd%d" % sl), ("ya%d" % sl,))

        def f2(q):
            sl = q % 2
            ob = 4 + (q % 2)
            obk = "pb%d" % ob
            Oc = pb[ob][:, 0:390].rearrange("p (h d) -> p h d", h=6)
            dc = st["dc"][:, sl * 6:(sl + 1) * 6]
            rdc = st["rdc"][:, sl * 6:(sl + 1) * 6]
            P.tt(dc, Oc[:, :, 64], esk[:, l * 6:(l + 1) * 6], ALU.add, (obk, "esk"), ("dc%d" % sl,))
            P.do("dve", lambda e: e.reciprocal(out=rdc, in_=dc), ("dc%d" % sl,), ("rdc%d" % sl,))
            P.tt(yc[sl].rearrange("p (h d) -> p h d", h=6), Oc[:, :, 0:64],
                 rdc.unsqueeze(2).to_broadcast([128, 6, 64]), ALU.mult, (obk, "rdc%d" % sl), ("yc%d" % sl,))
            ss2 = st["ss2"][:, sl * 2:(sl + 1) * 2]
            P.act(jk[sl], ya[sl], AF.Square, ("ya%d" % sl,), ("jk%d" % sl, "ssa%d" % sl), scale=384.0 ** -0.5, accum=ss2[:, 0:1])
            P.act(jk[sl], yc[sl], AF.Square, ("yc%d" % sl,), ("jk%d" % sl, "ssc%d" % sl), scale=384.0 ** -0.5, accum=ss2[:, 1:2])

        def f3(q):
            sl = q % 2
            q0 = q * 128
            ss2 = st["ss2"][:, sl * 2:(sl + 1) * 2]
            ln2 = st["ln2"][:, sl * 2:(sl + 1) * 2]
            rs2 = st["rs2"][:, sl * 2:(sl + 1) * 2]
            P.act(ln2, ss2, AF.Ln, ("ssa%d" % sl, "ssc%d" % sl, "epst"), ("ln2_%d" % sl,), bias=epst[:])
            P.act(rs2, ln2, AF.Exp, ("ln2_%d" % sl,), ("rs2_%d" % sl,), scale=-0.5)
            P.tsmul(y16[sl][:, 0:384], ya[sl], rs2[:, 0:1], ("ya%d" % sl, "rs2_%d" % sl), ("y16_%d" % sl,))
            P.tsmul(y16[sl][:, 384:768], yc[sl], rs2[:, 1:2], ("yc%d" % sl, "rs2_%d" % sl), ("y16_%d" % sl,))
            xk = "xres%d" % sl
            P.dma("sp", xres[sl], xsrc[q0:q0 + 128, :], xk, writes=(xk,))

        def f4(q):
            sl = q % 2
            yk = "y16_%d" % sl
            waits = P._waits((yk, "cst_b"), ("pT",))
            ev = None
            for k in range(6):
                ev = P.op("pe", (lambda e, k=k: e.transpose(out=pT[:, k * 128:(k + 1) * 128],
                                                            in_=y16[sl][:, k * 128:(k + 1) * 128], identity=identb)),
                          waits=waits if k == 0 else (), inc=True if k == 5 else None)
            P._commit(ev, (yk, "cst_b"), ("pT",))
            P.act(yT[sl], pT[:, 0:768].rearrange("p (k n) -> p k n", k=6), AF.Copy, ("pT",), ("yT%d" % sl,))

        def f5(q):
            sl = q % 2
            q0 = q * 128
            xk = "xres%d" % sl
            xok = "xo%d" % sl
            yt = yT[sl]
            lhs = [yt[:, 0, :], yt[:, 1, :], yt[:, 2, :], ybT[:, 0, q0:q0 + 128], ybT[:, 1, q0:q0 + 128],
                   yt[:, 3, :], yt[:, 4, :], yt[:, 5, :]]
            for half in range(2):
                bank = 2 + half
                bk = "pb%d" % bank
                mms = [(pb[bank][:, :], lhs[k], Wo[:, k, half * 512:(half + 1) * 512]) for k in range(8)]
                P.mm_group(mms, ("yT%d" % sl, "ybT", "Wo"), (bk,))
                P.tt(xo[sl][:, half * 512:(half + 1) * 512], pb[bank][:, :], xres[sl][:, half * 512:(half + 1) * 512],
                     ALU.add, (bk, xk), (xok,))
            P.dma("pool", xs[q0:q0 + 128, :], xo[sl], xok, reads=(xok,))

        stages = (f0, f1, f2, f3, f4, f5)
        for it in range(32 + 5):
            if it >= 6 and it % 4 == 2 and (it - 6) // 4 < 6:
                w1_prefetch((it - 6) // 4)
            for sg in (5, 4, 3, 2, 1, 0):
                q = it - sg
                if 0 <= q < 32:
                    stages[sg](q)
        P.barrier(dummy)
        if stop_after == (l, 3):
            break

        A0.top = 0
        A1.top = 0
        W1 = A0.alloc(8 * 4096).rearrange("p (k n) -> p k n", k=8)
        W2 = A0.alloc(32 * 1024).rearrange("p (k n) -> p k n", k=32)
        aT = A1.alloc(32 * 256).rearrange("p (k n) -> p k n", k=32)
        hT3s = [A1.alloc(8 * 256).rearrange("p (k n) -> p k n", k=8) for _ in range(2)]
        xin3 = [A1.f32(1024) for _ in range(4)]
        hb3s = [A1.b16(1024) for _ in range(2)]
        jk3 = A1.b16(1024)
        xo3 = [A1.f32(1024) for _ in range(2)]
        rl = [A1.f32(256) for _ in range(2)]
        last = (l == DEPTH - 1)
        dst_d = out_d if last else xs

        def p3_dma(T, s_):
            g = T * 2 + s_
            sl = g % 4
            P.dma("sp", xin3[sl], xs[g * 128:(g + 1) * 128, :], "xin3_%d" % sl, writes=("xin3_%d" % sl,))

        def p3_norm(T, s_):
            g = T * 2 + s_
            norm_tile(xin3[g % 4], "xin3_%d" % (g % 4), hb3s[g % 2], "hb3_%d" % (g % 2), g % 2)

        def p3_tr(T, s_):
            g = T * 2 + s_
            transpose8(hb3s[g % 2], "hb3_%d" % (g % 2), hT3s[T % 2], "hT3_%d" % (T % 2), s_ * 128)

        P.dma("sp", W1[:, 6:8, :], w1b[l][768:1024, :].rearrange("(k p) n -> p k n", p=128), "W1b", writes=("W1",))
        for s_ in range(2):
            p3_dma(0, s_)
        for kk in range(0, 32, 8):
            P.dma("sp", W2[:, kk:kk + 8, :], w2b[l][kk * 128:(kk + 8) * 128, :].rearrange("(k p) n -> p k n", p=128),
                  "W2b", writes=("W2",))
        for s_ in range(2):
            p3_norm(0, s_)
            p3_tr(0, s_)
        hctr = 0
        for T in range(16):
            hT3 = hT3s[T % 2]
            hk = "hT3_%d" % (T % 2)
            for f in range(32):
                bank = hctr % 3
                hctr += 1
                bk = "pb%d" % bank
                mms = [(pb[bank][:, 0:256], W1[:, k, f * 128:(f + 1) * 128], hT3[:, k, :]) for k in range(8)]
                P.mm_group(mms, ("W1", hk), (bk,))
                if T + 1 < 16:
                    if f == 0:
                        p3_dma(T + 1, 0)
                    if f == 6:
                        p3_dma(T + 1, 1)
                    if f == 3:
                        p3_norm(T + 1, 0)
                    if f == 10:
                        p3_norm(T + 1, 1)
                    if f == 8:
                        p3_tr(T + 1, 0)
                    if f == 15:
                        p3_tr(T + 1, 1)
                rs = f % 2
                rk = "rl%d" % rs
                P.act(rl[rs], pb[bank][:, 0:256], AF.Relu, (bk,), (rk,))
                P.tt(aT[:, f, :], rl[rs], rl[rs], ALU.mult, (rk,), ("aT",))
            for s in range(2):
                g = T * 2 + s
                sl = g % 4
                xk = "xin3_%d" % sl
                os_ = g % 2
                xok = "xo3_%d" % os_
                for half in range(2):
                    bank = 3 + ((g * 2 + half) % 4)
                    bk = "pb%d" % bank
                    mms = [(pb[bank][:, :], aT[:, f, s * 128:(s + 1) * 128], W2[:, f, half * 512:(half + 1) * 512])
                           for f in range(32)]
                    P.mm_group(mms, ("aT", "W2"), (bk,))
                    P.tt(xo3[os_][:, half * 512:(half + 1) * 512], pb[bank][:, :],
                         xin3[sl][:, half * 512:(half + 1) * 512], ALU.add, (bk, xk), (xok,))
                if last:
                    j = g % 2
                    ssq = st["fss"][:, j:j + 1]
                    lnv = st["fln"][:, j:j + 1]
                    rstd = st["frs"][:, j:j + 1]
                    P.act(jk3, xo3[os_], AF.Square, (xok,), ("jk3", "fss%d" % j), scale=1.0 / 32.0, accum=ssq)
                    P.act(lnv, ssq, AF.Ln, ("fss%d" % j, "epst"), ("fln%d" % j,), bias=epst[:])
                    P.act(rstd, lnv, AF.Exp, ("fln%d" % j,), ("frs%d" % j,), scale=-0.5)
                    P.tsmul(xin3[sl], xo3[os_], rstd, (xok, "frs%d" % j), (xk,))
                    P.tt(xo3[os_], xin3[sl], gfin[:, :], ALU.mult, (xk, "gfin"), (xok,))
                P.dma("pool", dst_d[g * 128:(g + 1) * 128, :], xo3[os_], xok, reads=(xok,))
        P.barrier(dummy)
    return P.build()


_CACHE = {}


def _consts():
    c = np.zeros((128, 768), np.float32)
    c[:, 0:128] = np.eye(128, dtype=np.float32)
    kj = np.arange(128)[:, None]
    qi = np.arange(128)[None, :]
    c[:, 128:256] = np.where(qi >= kj, 1.0, 0.0)
    c[:, 256:384] = np.where(qi <= kj, 1.0, 0.0)
    c[:, 384:512] = np.where(qi >= kj, 1.0, 0.0)
    c[:, 512:640] = np.where(qi < kj, 1.0, 0.0)
    return c


def _col_perm():
    idx = list(range(0, 384)) + list(range(384, 768))
    idx += list(range(1152, 1408)) + list(range(1408, 1664)) + list(range(1664, 1920))
    qc0 = 1920
    for c in range(3):
        idx += list(range(qc0 + c * 64, qc0 + (c + 1) * 64))
        idx += list(range(qc0 + (c + 3) * 64, qc0 + (c + 4) * 64))
    idx += list(range(2304, 2432))
    idx += list(range(768, 1152)) + list(range(2432, 2560))
    return np.array(idx)


def make_in_maps(x, w_in, conv_w, sinks, g_mix, g_group, w_o, g_mlp, w_ff_in, w_ff_out, g_final):
    f = lambda a: np.ascontiguousarray(np.asarray(a), dtype=np.float32)
    w_in_p = f(np.asarray(w_in)[:, :, _col_perm()])
    convw = f(np.asarray(conv_w).transpose(0, 2, 1).reshape(DEPTH, 2, 128, 3).transpose(0, 2, 1, 3))
    gm = lambda g: f(np.asarray(g).reshape(DEPTH, 8, 128).transpose(0, 2, 1))
    shared = {"w_in": w_in_p, "convw": convw, "sinks": f(np.asarray(sinks).reshape(DEPTH, 6)),
              "g_mix": gm(g_mix), "g_group": gm(g_group), "g_mlp": gm(g_mlp), "w_o": f(w_o),
              "w_ff_in": f(w_ff_in), "w_ff_out": f(w_ff_out), "g_final": f(g_final), "consts": _consts()}
    xa = np.asarray(x)
    return [dict(shared, x=f(xa[c])) for c in range(NCORES)]


def kernel(x, w_in, conv_w, sinks, g_mix, g_group, w_o, g_mlp, w_ff_in, w_ff_out, g_final):
    in_maps = make_in_maps(x, w_in, conv_w, sinks, g_mix, g_group, w_o, g_mlp, w_ff_in, w_ff_out, g_final)
    nc = build_program()
    res = run_bass_kernel_spmd(nc, in_maps, core_ids=list(range(NCORES)))
    return np.stack([np.asarray(r["out"], dtype=np.float32) for r in res.results], axis=0)
```

```python
from contextlib import ExitStack
import os
import numpy as np
import concourse.bass as bass
import concourse.mybir as mybir
from concourse.bass_utils import run_bass_kernel_spmd

F32 = mybir.dt.float32
BF16 = mybir.dt.bfloat16
AF = mybir.ActivationFunctionType
ALU = mybir.AluOpType

S = 4096
D = 1024
DEPTH = 2
NCORES = 8
A1CAP = 28672
EPS = 1e-6
ENGS = ("pe", "act", "dve", "pool", "sp")


class Sem:
    def __init__(self, h, name):
        self.h = h
        self.name = name
        self.count = 0


class Plan:
    def __init__(self):
        self.nc = bass.Bass("TRN2", target_bir_lowering=False)
        self.stack = ExitStack()
        self.ops = {e: [] for e in ENGS}
        self.waited = {e: {} for e in ENGS}
        self.prog = {e: self.sem("prog_" + e) for e in ENGS[:4]}
        self.dsem = {}
        self.w = {}
        self.r = {}
        self.pending_dma = {}
        self.last_ev = {}
        self.pe_rg = None
        self.sep_fn = None

    def sem(self, name):
        return Sem(self.stack.enter_context(self.nc.semaphore(name)), name)

    def sbuf(self, name, shape, dt):
        return self.stack.enter_context(self.nc.sbuf_tensor(name, list(shape), dt))

    def psum(self, name, shape, dt=F32):
        return self.stack.enter_context(self.nc.psum_tensor(name, list(shape), dt))

    def dram(self, name, shape, dt, kind="Internal"):
        return self.nc.dram_tensor(name, list(shape), dt, kind=kind).ap()

    def op(self, eng, fn, waits=(), inc=None, n=1, rg=None):
        if eng == "pe" and fn is not None:
            if rg is not None and self.pe_rg is not None and rg != self.pe_rg:
                self.ops["pe"].append(([], self.sep_fn, None))
            self.pe_rg = rg
        wl = []
        wd = self.waited[eng]
        for w in waits:
            if w is None:
                continue
            s, v = w
            if wd.get(s.name, 0) >= v:
                continue
            wd[s.name] = v
            wl.append((s.h, v))
        ev = None
        incs = None
        if inc is not None and inc is not False:
            s = self.prog[eng] if inc is True else inc
            s.count += n
            incs = (s.h, n)
            ev = (s, s.count)
            self.last_ev[eng] = ev
        self.ops[eng].append((wl, fn, incs))
        return ev

    def _waits(self, reads, writes):
        out = []
        for b in reads:
            if b in self.w:
                out.append(self.w[b])
        for b in writes:
            if b in self.w:
                out.append(self.w[b])
            out.extend(self.r.get(b, {}).values())
        return out

    def _commit(self, ev, reads, writes):
        for b in reads:
            d = self.r.setdefault(b, {})
            cur = d.get(ev[0].name)
            if cur is None or cur[1] < ev[1]:
                d[ev[0].name] = ev
        for b in writes:
            self.w[b] = ev
            self.r[b] = {}

    def do(self, eng, fn, reads=(), writes=(), extra=()):
        ev = self.op(eng, fn, waits=self._waits(reads, writes) + list(extra), inc=True)
        self._commit(ev, reads, writes)
        return ev

    def dma(self, eng, out, in_, semkey, reads=(), writes=()):
        if semkey not in self.dsem:
            self.dsem[semkey] = self.sem("d_" + semkey)
        s = self.dsem[semkey]
        ev = self.op(eng, lambda e, o=out, i=in_: e.dma_start(out=o, in_=i),
                     waits=self._waits(reads, writes), inc=s, n=16)
        self._commit(ev, reads, writes)
        self.pending_dma[s.name] = ev
        return ev

    def mm_group(self, mms, reads, writes, start_first=True, stop_last=True, inc=True):
        waits = self._waits(reads, writes)
        n = len(mms)
        ev = None
        for i, (o, l, r) in enumerate(mms):
            st = start_first and i == 0
            sp = stop_last and i == n - 1
            fn = (lambda e, o=o, l=l, r=r, st=st, sp=sp: e.matmul(o, lhsT=l, rhs=r, start=st, stop=sp))
            last = (i == n - 1) and inc
            ev = self.op("pe", fn, waits=waits if i == 0 else (), inc=True if last else None)
        if inc:
            self._commit(ev, reads, writes)
        return ev

    def pe_raw(self, fn, waits=(), inc=None):
        return self.op("pe", fn, waits=waits, inc=inc)

    def act(self, out, in_, func, reads, writes, scale=None, bias=None, accum=None):
        kw = {}
        if scale is not None:
            kw["scale"] = scale
        if bias is not None:
            kw["bias"] = bias
        if accum is not None:
            kw["accum_out"] = accum
        return self.do("act", lambda e: e.activation(out=out, in_=in_, func=func, **kw), reads, writes)

    def tcopy(self, eng, out, in_, reads, writes):
        return self.do(eng, lambda e: e.tensor_copy(out=out, in_=in_), reads, writes)

    def tt(self, out, in0, in1, op, reads, writes, eng="dve"):
        return self.do(eng, lambda e: e.tensor_tensor(out=out, in0=in0, in1=in1, op=op), reads, writes)

    def tsmul(self, out, in0, sc, reads, writes, eng="dve"):
        return self.do(eng, lambda e: e.tensor_scalar_mul(out=out, in0=in0, scalar1=sc), reads, writes)

    def stt(self, out, in0, sc, in1, op0, op1, reads, writes, eng="dve"):
        return self.do(eng, lambda e: e.scalar_tensor_tensor(out=out, in0=in0, scalar=sc, in1=in1, op0=op0, op1=op1),
                       reads, writes)

    def memset(self, eng, ap, val, writes):
        return self.do(eng, lambda e: e.memset(ap, val), (), writes)

    def barrier(self, dummy):
        evs = []
        evs.append(self.do("act", lambda e: e.activation(out=dummy["act"][:], in_=dummy["src"][:], func=AF.Copy),
                           ("epst",), ("dummy_act",)))
        evs.append(self.do("dve", lambda e: e.memset(dummy["dve"][:], 0.0), (), ("dummy_dve",)))
        if "pe" in self.last_ev:
            evs.append(self.last_ev["pe"])
        evs += list(self.pending_dma.values())
        self.pending_dma = {}
        for eng in ENGS:
            self.op(eng, None, waits=evs)

    def build(self):
        nc = self.nc
        with nc.Block() as block:
            def mk(engname):
                def body(e):
                    for wl, fn, incs in self.ops[engname]:
                        for h, v in wl:
                            e.wait_ge(h, v)
                        if fn is None:
                            continue
                        ins = fn(e)
                        if incs is not None:
                            ins.then_inc(incs[0], incs[1])
                return body
            block.tensor(mk("pe"))
            block.scalar(mk("act"))
            block.vector(mk("dve"))
            block.gpsimd(mk("pool"))
            block.sync(mk("sp"))
        self.stack.close()
        return nc


class Arena:
    def __init__(self, t, cap, name):
        self.t = t
        self.cap = cap
        self.top = 0
        self.name = name

    def alloc(self, nelem_bf16, dt=BF16):
        n = (nelem_bf16 + 15) // 16 * 16
        assert self.top + n <= self.cap, (self.name, self.top, n, self.cap)
        v = self.t[:, self.top:self.top + n]
        self.top += n
        if dt == F32:
            v = v.bitcast(F32)
        return v

    def f32(self, n):
        return self.alloc(2 * n, F32)[:, 0:n]

    def b16(self, n):
        return self.alloc(n)[:, 0:n]


def build_program(nlayers=DEPTH, debug=False, stop_after=None):
    P = Plan()
    nc = P.nc
    ext_in = "ExternalInput"
    dbg_kind = "ExternalOutput" if debug else "Internal"

    x_in = P.dram("x", [S, D], F32, ext_in)
    w_in_d = P.dram("w_in", [DEPTH, D, 2560], F32, ext_in)
    convw_d = P.dram("convw", [DEPTH, 128, 2, 3], F32, ext_in)
    sinks_d = P.dram("sinks", [DEPTH, 6], F32, ext_in)
    g_mix_d = P.dram("g_mix", [DEPTH, 128, 8], F32, ext_in)
    g_grp_d = P.dram("g_group", [DEPTH, 128, 8], F32, ext_in)
    g_mlp_d = P.dram("g_mlp", [DEPTH, 128, 8], F32, ext_in)
    w_o_d = P.dram("w_o", [DEPTH, D, D], F32, ext_in)
    w1_d = P.dram("w_ff_in", [DEPTH, D, 4096], F32, ext_in)
    w2_d = P.dram("w_ff_out", [DEPTH, 4096, D], F32, ext_in)
    g_fin_d = P.dram("g_final", [D], F32, ext_in)
    cst_d = P.dram("consts", [128, 768], F32, ext_in)
    out_d = P.dram("out", [S, D], F32, "ExternalOutput")

    xs = P.dram("xs", [S, D], F32, dbg_kind)
    Va = P.dram("Va", [S, 390], BF16, dbg_kind)
    Vc = P.dram("Vc", [S, 130], BF16, dbg_kind)
    Oa = P.dram("Oa", [3, S, 390], F32, dbg_kind)
    w1b = P.dram("w1b", [DEPTH, D, 4096], BF16)
    w2b = P.dram("w2b", [DEPTH, 4096, D], BF16)
    winb = P.dram("winb", [DEPTH, D, 2560], BF16)

    A0t = P.sbuf("A0", [128, 65536], BF16)
    A1t = P.sbuf("A1", [128, A1CAP], BF16)
    stage = [P.sbuf("stage%d" % i, [128, 1024], F32) for i in range(2)]
    sb16 = [P.sbuf("sb16_%d" % i, [128, 1024], BF16) for i in range(2)]
    cst_f = P.sbuf("cst_f", [128, 768], F32)
    cst_b = P.sbuf("cst_b", [128, 1664], BF16)
    onesb = P.sbuf("onesb", [128, 128], BF16)
    epst = P.sbuf("epst", [128, 1], F32)
    gfin = P.sbuf("gfin", [128, D], F32)
    gt = P.sbuf("gt", [128, DEPTH * 3 * 8], F32)
    cw = P.sbuf("cw", [128, DEPTH * 6], F32)
    sk = P.sbuf("sk", [128, DEPTH * 6], F32)
    esk = P.sbuf("esk", [128, DEPTH * 6], F32)
    dummy = {"act": P.sbuf("dmy_a", [128, 1], F32), "dve": P.sbuf("dmy_v", [128, 1], F32), "src": epst}
    st = {}
    for nm, w_ in (("ssq", 2), ("lnv", 2), ("rstd", 2), ("rd", 12), ("dc", 12), ("rdc", 12), ("ss2", 4), ("ln2", 4), ("rs2", 4),
                   ("fss", 2), ("fln", 2), ("frs", 2), ("ssa", 1), ("lna", 1), ("rsa", 1), ("ssc", 1), ("lnc", 1), ("rsc", 1)):
        st[nm] = P.sbuf("st_" + nm, [128, w_], F32)

    pT = P.psum("pT", [128, 1024], BF16)
    pball = P.psum("pball", [128, 7 * 512], F32)
    pb = [pball[:, i * 512:(i + 1) * 512] for i in range(7)]

    identb = cst_b[:, 0:128]
    maskA = cst_b[:, 640:1152]
    maskC = cst_b[:, 1152:1664]

    P.dma("sp", cst_f[:], cst_d[:, :], "cst", writes=("cst_f",))
    P.dma("sp", gfin[:], g_fin_d.partition_broadcast(128), "gfin", writes=("gfin",))
    for l in range(DEPTH):
        for k, gd in enumerate((g_mix_d, g_grp_d, g_mlp_d)):
            o = (l * 3 + k) * 8
            P.dma("sp", gt[:, o:o + 8], gd[l], "gt%d" % (l * 3 + k), writes=("gt%d_%d" % (l, k),))
        P.dma("sp", cw[:, l * 6:(l + 1) * 6], convw_d[l].rearrange("p c k -> p (c k)"), "cw%d" % l, writes=("cw%d" % l,))
        P.dma("sp", sk[:, l * 6:(l + 1) * 6], sinks_d[l].partition_broadcast(128), "sk%d" % l, writes=("sk%d" % l,))
    P.tcopy("dve", cst_b[:, 0:768], cst_f[:], ("cst_f",), ("cst_b0",))
    for mi in range(2):
        for hh in range(2):
            P.tcopy("dve", cst_b[:, 640 + mi * 512 + hh * 256:640 + mi * 512 + (hh + 1) * 256],
                    cst_f[:, 128 + mi * 256:384 + mi * 256], ("cst_f", "cst_b0"), ("cst_b%d" % (1 + mi * 2 + hh),))
    P.tcopy("dve", dummy["dve"][:], epst[:], ("cst_b0", "cst_b1", "cst_b2", "cst_b3", "cst_b4"), ("cst_b",))
    P.memset("dve", onesb[:], 1.0, ("onesb",))
    P.memset("dve", epst[:], EPS, ("epst",))
    P.act(esk[:], sk[:], AF.Exp, ("sk0", "sk1"), ("esk",))

    P.sep_fn = lambda e: e.matmul(pb[4][:, 448:512], lhsT=identb, rhs=cst_b[:, 0:64], start=True, stop=True)

    def gcol(l, kind, k):
        o = (l * 3 + kind) * 8 + k
        return gt[:, o:o + 1]

    def gkey(l, kind):
        return "gt%d_%d" % (l, kind)

    stage_ctr = [0]

    def load_weight(dst, src, K, N, dkey, gl=None, cols=None):
        for k in range(K):
            load_weight_k(dst, src, k, N, dkey, gl)

    def load_weight_k(dst, src, k, N, dkey, gl=None):
        if True:
            for c0 in range(0, N, 1024):
                cn = min(1024, N - c0)
                sl = stage_ctr[0] % 2
                stage_ctr[0] += 1
                skey = "stage%d" % sl
                P.dma("sp", stage[sl][:, 0:cn], src[k * 128:(k + 1) * 128, c0:c0 + cn], skey, writes=(skey,))
                if gl is not None:
                    P.tsmul(dst[:, k, c0:c0 + cn], stage[sl][:, 0:cn], gcol(gl[0], gl[1], k),
                            (skey, gkey(*gl)), (dkey,))
                else:
                    P.tcopy("dve", dst[:, k, c0:c0 + cn], stage[sl][:, 0:cn], (skey,), (dkey,))

    def precast_src(l, idx):
        if idx >= 100:
            k, cb = (idx - 100) // 3, (idx - 100) % 3
            c0, cn = ((0, 1024), (1024, 1024), (2048, 512))[cb]
            return (w_in_d[l][k * 128:(k + 1) * 128, c0:c0 + cn], winb[l][k * 128:(k + 1) * 128, c0:c0 + cn], (0, k, cn))
        if idx < 32:
            k, cb = idx // 4, idx % 4
            return (w1_d[l][k * 128:(k + 1) * 128, cb * 1024:(cb + 1) * 1024],
                    w1b[l][k * 128:(k + 1) * 128, cb * 1024:(cb + 1) * 1024], k)
        k = idx - 32
        return w2_d[l][k * 128:(k + 1) * 128, :], w2b[l][k * 128:(k + 1) * 128, :], None

    def precast_load(l, idx):
        src, dst, k = precast_src(l, idx)
        sl = stage_ctr[0] % 2
        stage_ctr[0] += 1
        cn = src.shape[1]
        P.dma("sp", stage[sl][:, 0:cn], src, "stage%d" % sl, writes=("stage%d" % sl,))
        return sl

    def precast_cast(l, idx, sl):
        src, dst, k = precast_src(l, idx)
        bs = idx % 2
        bk = "sb16_%d" % bs
        cn = src.shape[1]
        if isinstance(k, tuple):
            P.tsmul(sb16[bs][:, 0:cn], stage[sl][:, 0:cn], gcol(l, k[0], k[1]), ("stage%d" % sl, gkey(l, k[0])), (bk,))
        elif k is not None:
            P.tsmul(sb16[bs][:, :], stage[sl][:, :], gcol(l, 2, k), ("stage%d" % sl, gkey(l, 2)), (bk,))
        else:
            P.tcopy("dve", sb16[bs][:, :], stage[sl][:, :], ("stage%d" % sl,), (bk,))
        P.dma("pool", dst, sb16[bs][:, 0:cn], bk, reads=(bk,))

    def norm_tile(xt, xkey, hb, hbkey, j):
        ssq = st["ssq"][:, j:j + 1]
        lnv = st["lnv"][:, j:j + 1]
        rstd = st["rstd"][:, j:j + 1]
        P.act(hb, xt, AF.Square, (xkey,), (hbkey, "ssq%d" % j), scale=1.0 / 32.0, accum=ssq)
        P.act(lnv, ssq, AF.Ln, ("ssq%d" % j, "epst"), ("lnv%d" % j,), bias=epst[:])
        P.act(rstd, lnv, AF.Exp, ("lnv%d" % j,), ("rstd%d" % j,), scale=-0.5)
        P.tsmul(hb, xt, rstd, (xkey, "rstd%d" % j), (hbkey,))

    def transpose8(hb, hbkey, hT, hTkey, c0):
        waits = P._waits((hbkey, "cst_b"), ("pT",))
        ev = None
        for k in range(8):
            fn = (lambda e, k=k: e.transpose(out=pT[:, k * 128:(k + 1) * 128], in_=hb[:, k * 128:(k + 1) * 128],
                                             identity=identb))
            ev = P.op("pe", fn, waits=waits if k == 0 else (), inc=True if k == 7 else None)
        P._commit(ev, (hbkey, "cst_b"), ("pT",))
        P.tcopy("dve", hT[:, :, c0:c0 + 128], pT[:, :].rearrange("p (k n) -> p k n", k=8), ("pT",), (hTkey,))

    for l in range(nlayers):
        xsrc = x_in if l == 0 else xs
        A0 = Arena(A0t, 65536, "A0")
        A1 = Arena(A1t, A1CAP, "A1")
        QaT = A0.alloc(3 * S).rearrange("p (c n) -> p c n", c=3)
        KaT = A0.alloc(3 * S).rearrange("p (c n) -> p c n", c=3)
        QcT = A0.alloc(3 * S).rearrange("p (c n) -> p c n", c=3)
        KcT = A0.alloc(S)
        ybT = A0.alloc(2 * S).rearrange("p (c n) -> p c n", c=2)
        a0_mark = A0.top

        Win = A1.alloc(8 * 2560).rearrange("p (k n) -> p k n", k=8)
        hTs = [A1.alloc(8 * 512).rearrange("p (k n) -> p k n", k=8) for _ in range(2)]
        xin = [A0.f32(1024) for _ in range(2)]
        hbs = [A0.b16(1024), A0.b16(1024)]
        gcs = A0.f32(512)
        u = [A0.f32(516) for _ in range(2)]
        acc = A0.f32(512)
        yb = [A0.f32(512) for _ in range(2)]
        sq = [A0.b16(512) for _ in range(2)]
        rsb = A0.f32(512)
        Vst = [A0.b16(520).rearrange("p (h d) -> p h d", h=8) for _ in range(2)]

        if l == 0:
            load_weight(Win, w_in_d[l], 8, 2560, "Win", gl=(l, 0))
        else:
            for kk in range(0, 8, 2):
                P.dma("sp", Win[:, kk:kk + 2, :], winb[l][kk * 128:(kk + 2) * 128, :].rearrange("(k p) n -> p k n", p=128),
                      "Winb", writes=("Win",))
        for j in range(2):
            P.memset("dve", u[j][:, 0:2], 0.0, ("u%d" % j,))
            P.memset("dve", Vst[j][:, :, 64:65], 1.0, ("Vst%d" % j,))

        def p1_dma(T, s_):
            g = T * 4 + s_
            sl = g % 2
            xk = "xin%d" % sl
            P.dma("sp", xin[sl][:], xsrc[g * 128:(g + 1) * 128, :], xk, writes=(xk,))

        def p1_norm(T, s_):
            g = T * 4 + s_
            sl = g % 2
            norm_tile(xin[sl][:], "xin%d" % sl, hbs[sl], "hb%d" % sl, sl)

        def p1_tr(T, s_):
            g = T * 4 + s_
            sl = g % 2
            transpose8(hbs[sl], "hb%d" % sl, hTs[T % 2], "hT%d" % (T % 2), s_ * 128)

        for s_ in range(4):
            p1_dma(0, s_)
            p1_norm(0, s_)
            p1_tr(0, s_)
        pc_next = [0]
        pc_pending = [None]
        for T in range(8):
            t0 = T * 512
            hT = hTs[T % 2]
            hTk = "hT%d" % (T % 2)
            gctr = [0]

            def tick():
                gi = gctr[0]
                gctr[0] += 1
                if gi in (1, 3, 6, 8, 11, 13, 16, 18):
                    if pc_pending[0] is not None:
                        precast_cast(l, pc_pending[0][0], pc_pending[0][1])
                        pc_pending[0] = None
                    if pc_next[0] < 64:
                        pc_pending[0] = (pc_next[0], precast_load(l, pc_next[0]))
                        pc_next[0] += 1
                if T + 1 >= 8:
                    return
                if gi in (0, 2, 5, 8):
                    p1_dma(T + 1, (0, 2, 5, 8).index(gi))
                if gi in (1, 4, 7, 10):
                    p1_norm(T + 1, (gi - 1) // 3)
                if gi in (4, 7, 10, 13):
                    p1_tr(T + 1, (gi - 4) // 3)
            def fm_chunk(ci, bank):
                bk = "pb%d" % bank
                mms = [(pb[bank][:, :], Win[:, k, ci * 128:(ci + 1) * 128], hT[:, k, :]) for k in range(8)]
                P.mm_group(mms, ("Win", hTk), (bk,))
                tick()
                return bk
            zi = 0
            for ci, dst, dkey in ([(c, QaT[:, c, t0:t0 + 512], "QaT") for c in range(3)] +
                                  [(3 + c, KaT[:, c, t0:t0 + 512], "KaT") for c in range(3)] +
                                  [(12 + c, QcT[:, c, t0:t0 + 512], "QcT") for c in range(3)] +
                                  [(15, KcT[:, t0:t0 + 512], "KcT")]):
                bank = zi % 3
                zi += 1
                bk = fm_chunk(ci, bank)
                P.act(dst, pb[bank][:, :], AF.Copy, (bk,), (dkey,))
            for j in range(2):
                bank = zi % 3; zi += 1
                bk = fm_chunk(8 + j, bank)
                P.act(gcs, pb[bank][:, :], AF.Copy, (bk,), ("gcs",))
                bank = zi % 3; zi += 1
                bk = fm_chunk(10 + j, bank)
                uk = "u%d" % j
                P.tt(u[j][:, 2:514], pb[bank][:, :], gcs, ALU.mult, (bk, "gcs"), (uk,))
                wofs = l * 6 + j * 3
                P.tsmul(acc, u[j][:, 0:512], cw[:, wofs:wofs + 1], (uk, "cw%d" % l), ("acc",))
                P.stt(acc, u[j][:, 1:513], cw[:, wofs + 1:wofs + 2], acc, ALU.mult, ALU.add, (uk, "cw%d" % l, "acc"), ("acc",))
                P.stt(acc, u[j][:, 2:514], cw[:, wofs + 2:wofs + 3], acc, ALU.mult, ALU.add, (uk, "cw%d" % l, "acc"), ("acc",))
                bank = zi % 3; zi += 1
                bk = fm_chunk(6 + j, bank)
                P.tt(yb[j], pb[bank][:, :], acc, ALU.mult, (bk, "acc"), ("yb%d" % j,))
                P.act(sq[j], yb[j], AF.Square, ("yb%d" % j,), ("sq%d" % j,))
                P.tcopy("dve", u[j][:, 0:2], u[j][:, 512:514], (uk,), (uk,))
            for s in range(4):
                g = T * 4 + s
                bank = 3 + (g % 2)
                bk = "pb%d" % bank
                mms = [(pb[bank][:, :], hT[:, k, s * 128:(s + 1) * 128], Win[:, k, 2048:2560]) for k in range(8)]
                P.mm_group(mms, ("Win", hTk), (bk,))
                tick()
                vs = g % 2
                vk = "Vst%d" % vs
                P.tcopy("dve", Vst[vs][:, :, 0:64], pb[bank][:, :].rearrange("p (h d) -> p h d", h=8), (bk,), (vk,))
                P.dma("pool", Va[g * 128:(g + 1) * 128, :], Vst[vs][:, 0:6, :].rearrange("p h d -> p (h d)"), vk, reads=(vk,))
                P.dma("pool", Vc[g * 128:(g + 1) * 128, :], Vst[vs][:, 6:8, :].rearrange("p h d -> p (h d)"), vk, reads=(vk,))
            P.mm_group([(pb[5][:, :], onesb[:, :], sq[0]), (pb[5][:, :], onesb[:, :], sq[1])],
                       ("onesb", "sq0", "sq1"), ("pb5",))
            tick()
            P.act(acc, pb[5][:, :], AF.Ln, ("pb5", "epst"), ("acc",), scale=1.0 / 256.0, bias=epst[:])
            P.act(rsb, acc, AF.Exp, ("acc",), ("rsb",), scale=-0.5)
            for j in range(2):
                P.tt(ybT[:, j, t0:t0 + 512], yb[j], rsb, ALU.mult, ("yb%d" % j, "rsb"), ("ybT",))
        if pc_pending[0] is not None:
            precast_cast(l, pc_pending[0][0], pc_pending[0][1])
        assert pc_next[0] == 64
        P.barrier(dummy)
        if stop_after == (l, 1):
            break

        A0.top = a0_mark
        A1.top = 0
        Wo = A1.alloc(8 * 1024).rearrange("p (k n) -> p k n", k=8)
        Vcall = A1.alloc(32 * 130).rearrange("p (j h d) -> p j h d", j=32, h=2)
        PT = [A1.alloc(3 * 512).rearrange("p (c n) -> p c n", c=3) for _ in range(3)]
        Vr = [A1.alloc(392)[:, 0:390].rearrange("p (h d) -> p h d", h=6) for _ in range(4)]
        Ost = [A1.f32(390) for _ in range(2)]
        xo = [A1.f32(1024) for _ in range(2)]
        OaL = [A0.f32(3 * 390).rearrange("p (n c) -> p n c", n=3) for _ in range(2)]
        N1 = A0.f32(390)
        Ns = A0.f32(390)
        ya = [A0.f32(384) for _ in range(2)]
        yc = [A0.f32(384) for _ in range(2)]
        y16 = [A0.b16(768) for _ in range(2)]
        jk = [A0.b16(384) for _ in range(2)]
        yT = [A0.b16(768).rearrange("p (c n) -> p c n", c=6) for _ in range(2)]
        xres = [A1.f32(1024) for _ in range(2)]
        PT.append(A0.alloc(3 * 512).rearrange("p (c n) -> p c n", c=3))

        P.dma("sp", Vcall.rearrange("p j h d -> p j (h d)"), Vc.rearrange("(j p) c -> p j c", p=128), "Vcall",
              writes=("Vcall",))

        sctr = [0]
        octr = [0]
        vctr = [0]
        pctr = [0]

        def score_mm(bank, hh, kt, qt, nq):
            bk = "pb%d" % bank
            o = pb[bank][:, hh * 256:hh * 256 + nq]
            ev = P.op("pe", (lambda e, o=o, l_=kt, r_=qt: e.matmul(o, lhsT=l_, rhs=r_, start=True, stop=True)),
                      waits=P._waits(("QK",), (bk,)), inc=True, rg=hh)
            P._commit(ev, ("QK",), (bk,))

        def score_exp(bank, mask, nq, pslot, c, meng="dve"):
            bk = "pb%d" % bank
            pk = "PT%d_%d" % (pslot, c)
            src = pb[bank][:, :].rearrange("p (h n) -> p h n", h=2)[:, :, 0:nq]
            dst = PT[pslot][:, c, :].rearrange("p (h n) -> p h n", h=2)[:, :, 0:nq]
            P.act(dst, src, AF.Exp, (bk,), (pk,), scale=0.125)
            P.tt(dst, dst, mask.rearrange("p (h n) -> p h n", h=2)[:, :, 0:nq], ALU.mult, (pk, "cst_b"), (pk,), eng=meng)

        SB = (0, 1, 2, 3)
        items = []
        for pi, dil in enumerate((1, 4, 16)):
            nb = 32 // dil
            for r in range(dil):
                for kb in range(nb):
                    items.append((pi, dil, r, kb, nb))
        slots = {}

        def a_info(i):
            pi, dil, r, kb, nb = items[i]
            nq = 256 if kb < nb - 1 else 128
            k0 = r + dil * 128 * kb
            return pi, dil, r, kb, nb, nq, k0

        def a_loadv(i):
            pi, dil, r, kb, nb, nq, k0 = a_info(i)
            Vav = Va.rearrange("(j p dl) c -> dl p j c", p=128, dl=dil)
            vk = "Vr%d" % (i % 4)
            P.dma("sp", Vr[i % 4].rearrange("p h d -> p (h d)"), Vav[r][:, kb, :], vk, writes=(vk,))

        def a_scores_half(i, hh):
            pi, dil, r, kb, nb, nq, k0 = a_info(i)
            for c in range(3):
                score_mm(SB[(3 * i + c) % 4], hh, KaT[hh * 64:(hh + 1) * 64, c, k0:k0 + dil * 127 + 1:dil],
                         QaT[hh * 64:(hh + 1) * 64, c, k0:k0 + dil * (nq - 1) + 1:dil], nq)

        def a_exp(i):
            pi, dil, r, kb, nb, nq, k0 = a_info(i)
            for c in range(3):
                score_exp(SB[(3 * i + c) % 4], maskA, nq, i % 4, c)

        def a_pv(i, part):
            pi, dil, r, kb, nb = items[i]
            Oav = Oa[pi].rearrange("(j p dl) c -> dl p j c", p=128, dl=dil)
            vs, ps = i % 4, i % 4
            pvs, pps = (i - 1) % 4, (i - 1) % 4
            ob = 4 + (i % 2)
            obk = "pb%d" % ob
            rd_ = ["Vr%d" % vs] + ["PT%d_%d" % (ps, c) for c in range(3)]
            if kb > 0:
                rd_ += ["Vr%d" % pvs] + ["PT%d_%d" % (pps, c) for c in range(3)]
            waits = P._waits(rd_, (obk,))
            ev = None
            first = True
            hs = (0, 1, 2) if part == 0 else (3, 4, 5)
            for h in hs:
                c, hh = h // 2, h % 2
                o = pb[ob][:, h * 65:(h + 1) * 65]
                if kb > 0:
                    P.op("pe", (lambda e, o=o, l_=PT[pps][:, c, hh * 256 + 128:hh * 256 + 256], r_=Vr[pvs][:, h, :]:
                                e.matmul(o, lhsT=l_, rhs=r_, start=True, stop=False)),
                         waits=waits if first else ())
                    first = False
                ev = P.op("pe", (lambda e, o=o, l_=PT[ps][:, c, hh * 256:hh * 256 + 128], r_=Vr[vs][:, h, :], st_=(kb == 0):
                                 e.matmul(o, lhsT=l_, rhs=r_, start=st_, stop=True)),
                          waits=waits if first else (), inc=True if h == hs[-1] else None)
                first = False
            P._commit(ev, rd_, (obk,))
            if part == 1:
                os_ = i % 2
                ok = "Ost%d" % os_
                P.tcopy("dve", Ost[os_], pb[ob][:, 0:390], (obk,), (ok,))
                P.dma("pool", Oav[r][:, kb, :], Ost[os_], ok, reads=(ok,))

        LA = 2
        for j in range(LA):
            a_loadv(j)
            a_scores_half(j, 0)
            a_scores_half(j, 1)
            a_exp(j)
        for i in range(len(items)):
            nxt = i + LA < len(items)
            if i % 8 == 2 and i // 8 < 8:
                load_weight_k(Wo, w_o_d[l], i // 8, 1024, "Wo", gl=(l, 1))
            if nxt:
                a_loadv(i + LA)
                a_scores_half(i + LA, 0)
            a_pv(i, 0)
            if nxt:
                a_scores_half(i + LA, 1)
                a_exp(i + LA)
            a_pv(i, 1)
        pctr[0] = len(items)
        octr[0] = len(items)
        P.barrier(dummy)
        if stop_after == (l, 2):
            break

        CB = (0, 1, 6)
        PB = 96
        W1pre = A0t[:, 0:8 * 4096].rearrange("p (k n) -> p k n", k=8)

        def w1_prefetch(kk):
            P.dma("sp", W1pre[:, kk:kk + 1, :], w1b[l][kk * 128:(kk + 1) * 128, :].rearrange("(k p) n -> p k n", p=128),
                  "W1pre", writes=("W1",))

        def f0(q):
            q0 = q * 128
            nq = 256 if q < 31 else 128
            sl = q % 2
            lk = "OaL%d" % sl
            P.dma("sp", OaL[sl], Oa[:, q0:q0 + 128, :].rearrange("n t c -> t n c"), lk, writes=(lk,))
            for hh in range(2):
                for c in range(3):
                    score_mm(CB[c], hh, KcT[hh * 64:(hh + 1) * 64, q0:q0 + 128], QcT[hh * 64:(hh + 1) * 64, c, q0:q0 + nq], nq)
            for c in range(3):
                score_exp(CB[c], maskC, nq, (PB + q) % 4, c, meng="dve")

        def f1(q):
            ps = (PB + q) % 4
            pps = (PB + q - 1) % 4
            ob = 4 + (q % 2)
            obk = "pb%d" % ob
            rd_ = ["Vcall"] + ["PT%d_%d" % (ps, c) for c in range(3)]
            if q > 0:
                rd_ += ["PT%d_%d" % (pps, c) for c in range(3)]
            waits = P._waits(rd_, (obk,))
            ev = None
            first = True
            for h in range(6):
                hh, c = h // 3, h % 3
                o = pb[ob][:, h * 65:(h + 1) * 65]
                if q > 0:
                    P.op("pe", (lambda e, o=o, l_=PT[pps][:, c, hh * 256 + 128:hh * 256 + 256], r_=Vcall[:, q - 1, hh, :]:
                                e.matmul(o, lhsT=l_, rhs=r_, start=True, stop=False)),
                         waits=waits if first else ())
                    first = False
                ev = P.op("pe", (lambda e, o=o, l_=PT[ps][:, c, hh * 256:hh * 256 + 128], r_=Vcall[:, q, hh, :], st_=(q == 0):
                                 e.matmul(o, lhsT=l_, rhs=r_, start=st_, stop=True)),
                          waits=waits if first else (), inc=True if h == 5 else None)
                first = False
            P._commit(ev, rd_, (obk,))
            sl = q % 2
            lk = "OaL%d" % sl
            P.tt(N1, OaL[sl][:, 0, :], OaL[sl][:, 1, :], ALU.add, (lk,), ("N1",), eng="dve")
            P.tt(Ns, N1, OaL[sl][:, 2, :], ALU.add, (lk, "N1"), ("Ns",), eng="dve")
            Nv = Ns.rearrange("p (h d) -> p h d", h=6)
            rd = st["rd"][:, sl * 6:(sl + 1) * 6]
            P.do("dve", lambda e: e.reciprocal(out=rd, in_=Nv[:, :, 64]), ("Ns",), ("rd%d" % sl,))
            P.tt(ya[sl].rearrange("p (h d) -> p h d", h=6), Nv[:, :, 0:64],
                 rd.unsqueeze(2).to_broadcast([128, 6, 64]), ALU.mult, ("Ns", "rd%d" % sl), ("ya%d" % sl,))

        def f2(q):
            sl = q % 2
            ob = 4 + (q % 2)
            obk = "pb%d" % ob
            Oc = pb[ob][:, 0:390].rearrange("p (h d) -> p h d", h=6)
            dc = st["dc"][:, sl * 6:(sl + 1) * 6]
            rdc = st["rdc"][:, sl * 6:(sl + 1) * 6]
            P.tt(dc, Oc[:, :, 64], esk[:, l * 6:(l + 1) * 6], ALU.add, (obk, "esk"), ("dc%d" % sl,))
            P.do("dve", lambda e: e.reciprocal(out=rdc, in_=dc), ("dc%d" % sl,), ("rdc%d" % sl,))
            P.tt(yc[sl].rearrange("p (h d) -> p h d", h=6), Oc[:, :, 0:64],
                 rdc.unsqueeze(2).to_broadcast([128, 6, 64]), ALU.mult, (obk, "rdc%d" % sl), ("yc%d" % sl,))
            ss2 = st["ss2"][:, sl * 2:(sl + 1) * 2]
            P.act(jk[sl], ya[sl], AF.Square, ("ya%d" % sl,), ("jk%d" % sl, "ssa%d" % sl), scale=384.0 ** -0.5, accum=ss2[:, 0:1])
            P.act(jk[sl], yc[sl], AF.Square, ("yc%d" % sl,), ("jk%d" % sl, "ssc%d" % sl), scale=384.0 ** -0.5, accum=ss2[:, 1:2])

        def f3(q):
            sl = q % 2
            q0 = q * 128
            ss2 = st["ss2"][:, sl * 2:(sl + 1) * 2]
            ln2 = st["ln2"][:, sl * 2:(sl + 1) * 2]
            rs2 = st["rs2"][:, sl * 2:(sl + 1) * 2]
            P.act(ln2, ss2, AF.Ln, ("ssa%d" % sl, "ssc%d" % sl, "epst"), ("ln2_%d" % sl,), bias=epst[:])
            P.act(rs2, ln2, AF.Exp, ("ln2_%d" % sl,), ("rs2_%d" % sl,), scale=-0.5)
            P.tsmul(y16[sl][:, 0:384], ya[sl], rs2[:, 0:1], ("ya%d" % sl, "rs2_%d" % sl), ("y16_%d" % sl,))
            P.tsmul(y16[sl][:, 384:768], yc[sl], rs2[:, 1:2], ("yc%d" % sl, "rs2_%d" % sl), ("y16_%d" % sl,))
            xk = "xres%d" % sl
            P.dma("sp", xres[sl], xsrc[q0:q0 + 128, :], xk, writes=(xk,))

        def f4(q):
            sl = q % 2
            yk = "y16_%d" % sl
            waits = P._waits((yk, "cst_b"), ("pT",))
            ev = None
            for k in range(6):
                ev = P.op("pe", (lambda e, k=k: e.transpose(out=pT[:, k * 128:(k + 1) * 128],
                                                            in_=y16[sl][:, k * 128:(k + 1) * 128], identity=identb)),
                          waits=waits if k == 0 else (), inc=True if k == 5 else None)
            P._commit(ev, (yk, "cst_b"), ("pT",))
            P.act(yT[sl], pT[:, 0:768].rearrange("p (k n) -> p k n", k=6), AF.Copy, ("pT",), ("yT%d" % sl,))

        def f5(q):
            sl = q % 2
            q0 = q * 128
            xk = "xres%d" % sl
            xok = "xo%d" % sl
            yt = yT[sl]
            lhs = [yt[:, 0, :], yt[:, 1, :], yt[:, 2, :], ybT[:, 0, q0:q0 + 128], ybT[:, 1, q0:q0 + 128],
                   yt[:, 3, :], yt[:, 4, :], yt[:, 5, :]]
            for half in range(2):
                bank = 2 + half
                bk = "pb%d" % bank
                mms = [(pb[bank][:, :], lhs[k], Wo[:, k, half * 512:(half + 1) * 512]) for k in range(8)]
                P.mm_group(mms, ("yT%d" % sl, "ybT", "Wo"), (bk,))
                P.tt(xo[sl][:, half * 512:(half + 1) * 512], pb[bank][:, :], xres[sl][:, half * 512:(half + 1) * 512],
                     ALU.add, (bk, xk), (xok,))
            P.dma("pool", xs[q0:q0 + 128, :], xo[sl], xok, reads=(xok,))

        stages = (f0, f1, f2, f3, f4, f5)
        for it in range(32 + 5):
            if it >= 6 and it % 4 == 2 and (it - 6) // 4 < 6:
                w1_prefetch((it - 6) // 4)
            for sg in (5, 4, 3, 2, 1, 0):
                q = it - sg
                if 0 <= q < 32:
                    stages[sg](q)
        P.barrier(dummy)
        if stop_after == (l, 3):
            break

        A0.top = 0
        A1.top = 0
        W1 = A0.alloc(8 * 4096).rearrange("p (k n) -> p k n", k=8)
        W2 = A0.alloc(32 * 1024).rearrange("p (k n) -> p k n", k=32)
        aT = A1.alloc(32 * 256).rearrange("p (k n) -> p k n", k=32)
        hT3s = [A1.alloc(8 * 256).rearrange("p (k n) -> p k n", k=8) for _ in range(2)]
        xin3 = [A1.f32(1024) for _ in range(4)]
        hb3s = [A1.b16(1024) for _ in range(2)]
        jk3 = A1.b16(1024)
        xo3 = [A1.f32(1024) for _ in range(2)]
        rl = [A1.f32(256) for _ in range(2)]
        last = (l == DEPTH - 1)
        dst_d = out_d if last else xs

        def p3_dma(T, s_):
            g = T * 2 + s_
            sl = g % 4
            P.dma("sp", xin3[sl], xs[g * 128:(g + 1) * 128, :], "xin3_%d" % sl, writes=("xin3_%d" % sl,))

        def p3_norm(T, s_):
            g = T * 2 + s_
            norm_tile(xin3[g % 4], "xin3_%d" % (g % 4), hb3s[g % 2], "hb3_%d" % (g % 2), g % 2)

        def p3_tr(T, s_):
            g = T * 2 + s_
            transpose8(hb3s[g % 2], "hb3_%d" % (g % 2), hT3s[T % 2], "hT3_%d" % (T % 2), s_ * 128)

        P.dma("sp", W1[:, 6:8, :], w1b[l][768:1024, :].rearrange("(k p) n -> p k n", p=128), "W1b", writes=("W1",))
        for s_ in range(2):
            p3_dma(0, s_)
        for kk in range(0, 32, 8):
            P.dma("sp", W2[:, kk:kk + 8, :], w2b[l][kk * 128:(kk + 8) * 128, :].rearrange("(k p) n -> p k n", p=128),
                  "W2b%d" % (kk // 8), writes=("W2_%d" % (kk // 8),))
        for s_ in range(2):
            p3_norm(0, s_)
            p3_tr(0, s_)
        hctr = 0
        wp_next = [0]
        wp_pending = [None]
        for T in range(16):
            hT3 = hT3s[T % 2]
            hk = "hT3_%d" % (T % 2)
            for f in range(32):
                bank = hctr % 3
                hctr += 1
                bk = "pb%d" % bank
                mms = [(pb[bank][:, 0:256], W1[:, k, f * 128:(f + 1) * 128], hT3[:, k, :]) for k in range(8)]
                P.mm_group(mms, ("W1", hk), (bk,))
                if l + 1 < nlayers and f in (20, 28):
                    if wp_pending[0] is not None:
                        precast_cast(l + 1, wp_pending[0][0], wp_pending[0][1])
                        wp_pending[0] = None
                    if wp_next[0] < 24:
                        wp_pending[0] = (100 + wp_next[0], precast_load(l + 1, 100 + wp_next[0]))
                        wp_next[0] += 1
                if T + 1 < 16:
                    if f == 0:
                        p3_dma(T + 1, 0)
                    if f == 6:
                        p3_dma(T + 1, 1)
                    if f == 3:
                        p3_norm(T + 1, 0)
                    if f == 10:
                        p3_norm(T + 1, 1)
                    if f == 8:
                        p3_tr(T + 1, 0)
                    if f == 15:
                        p3_tr(T + 1, 1)
                rs = f % 2
                rk = "rl%d" % rs
                P.act(rl[rs], pb[bank][:, 0:256], AF.Relu, (bk,), (rk,))
                P.tt(aT[:, f, :], rl[rs], rl[rs], ALU.mult, (rk,), ("aT",))
            for s in range(2):
                g = T * 2 + s
                sl = g % 4
                xk = "xin3_%d" % sl
                os_ = g % 2
                xok = "xo3_%d" % os_
                for half in range(2):
                    bank = 3 + ((g * 2 + half) % 4)
                    bk = "pb%d" % bank
                    for qi in range(4):
                        mms = [(pb[bank][:, :], aT[:, f, s * 128:(s + 1) * 128], W2[:, f, half * 512:(half + 1) * 512])
                               for f in range(qi * 8, qi * 8 + 8)]
                        P.mm_group(mms, ("aT", "W2_%d" % qi), (bk,), start_first=(qi == 0), stop_last=(qi == 3), inc=(qi == 3))
                    P.tt(xo3[os_][:, half * 512:(half + 1) * 512], pb[bank][:, :],
                         xin3[sl][:, half * 512:(half + 1) * 512], ALU.add, (bk, xk), (xok,))
                if last:
                    j = g % 2
                    ssq = st["fss"][:, j:j + 1]
                    lnv = st["fln"][:, j:j + 1]
                    rstd = st["frs"][:, j:j + 1]
                    P.act(jk3, xo3[os_], AF.Square, (xok,), ("jk3", "fss%d" % j), scale=1.0 / 32.0, accum=ssq)
                    P.act(lnv, ssq, AF.Ln, ("fss%d" % j, "epst"), ("fln%d" % j,), bias=epst[:])
                    P.act(rstd, lnv, AF.Exp, ("fln%d" % j,), ("frs%d" % j,), scale=-0.5)
                    P.tsmul(xin3[sl], xo3[os_], rstd, (xok, "frs%d" % j), (xk,))
                    P.tt(xo3[os_], xin3[sl], gfin[:, :], ALU.mult, (xk, "gfin"), (xok,))
                P.dma("pool", dst_d[g * 128:(g + 1) * 128, :], xo3[os_], xok, reads=(xok,))
        if wp_pending[0] is not None:
            precast_cast(l + 1, wp_pending[0][0], wp_pending[0][1])
        P.barrier(dummy)
    return P.build()


_CACHE = {}


def _consts():
    c = np.zeros((128, 768), np.float32)
    c[:, 0:128] = np.eye(128, dtype=np.float32)
    kj = np.arange(128)[:, None]
    qi = np.arange(128)[None, :]
    c[:, 128:256] = np.where(qi >= kj, 1.0, 0.0)
    c[:, 256:384] = np.where(qi <= kj, 1.0, 0.0)
    c[:, 384:512] = np.where(qi >= kj, 1.0, 0.0)
    c[:, 512:640] = np.where(qi < kj, 1.0, 0.0)
    return c


def _col_perm():
    idx = list(range(0, 384)) + list(range(384, 768))
    idx += list(range(1152, 1408)) + list(range(1408, 1664)) + list(range(1664, 1920))
    qc0 = 1920
    for c in range(3):
        idx += list(range(qc0 + c * 64, qc0 + (c + 1) * 64))
        idx += list(range(qc0 + (c + 3) * 64, qc0 + (c + 4) * 64))
    idx += list(range(2304, 2432))
    idx += list(range(768, 1152)) + list(range(2432, 2560))
    return np.array(idx)


def make_in_maps(x, w_in, conv_w, sinks, g_mix, g_group, w_o, g_mlp, w_ff_in, w_ff_out, g_final):
    f = lambda a: np.ascontiguousarray(np.asarray(a), dtype=np.float32)
    w_in_p = f(np.asarray(w_in)[:, :, _col_perm()])
    convw = f(np.asarray(conv_w).transpose(0, 2, 1).reshape(DEPTH, 2, 128, 3).transpose(0, 2, 1, 3))
    gm = lambda g: f(np.asarray(g).reshape(DEPTH, 8, 128).transpose(0, 2, 1))
    shared = {"w_in": w_in_p, "convw": convw, "sinks": f(np.asarray(sinks).reshape(DEPTH, 6)),
              "g_mix": gm(g_mix), "g_group": gm(g_group), "g_mlp": gm(g_mlp), "w_o": f(w_o),
              "w_ff_in": f(w_ff_in), "w_ff_out": f(w_ff_out), "g_final": f(g_final), "consts": _consts()}
    xa = np.asarray(x)
    return [dict(shared, x=f(xa[c])) for c in range(NCORES)]


def kernel(x, w_in, conv_w, sinks, g_mix, g_group, w_o, g_mlp, w_ff_in, w_ff_out, g_final):
    in_maps = make_in_maps(x, w_in, conv_w, sinks, g_mix, g_group, w_o, g_mlp, w_ff_in, w_ff_out, g_final)
    nc = build_program()
    res = run_bass_kernel_spmd(nc, in_maps, core_ids=list(range(NCORES)))
    return np.stack([np.asarray(r["out"], dtype=np.float32) for r in res.results], axis=0)
```
